# Optimizing a Trainium2 kernel written in Bass

```python
import math
import jax, jax.numpy as jnp
from jax import lax
import numpy as np

D_MODEL = 1024
BATCH = 2
SEQ = 8192
DEPTH = 2

HEAD_DIM = 64
Q_BLOCK = 128
NSA_HEADS = 8
NSA_KV_HEADS = 2
CMP_LEN = 32
CMP_STRIDE = 16
CMP_HIDDEN = 256
SLC_LEN = 64
SLC_TOPN = 16
NSA_WINDOW = 512
FORCE_SCORE = 1e4
SB_HEADS = 4
SWA_HEADS = 4
SWA_KV_HEADS = 2
SWA_WINDOW = 128
NUM_BUCKETS = 32
MAX_DISTANCE = 128
N_BIAS_HEADS = NSA_HEADS + SWA_HEADS
D_FF = 2816
NORM_EPS = 1e-6
NEG_INF = -1e30
N_BRANCHES = 3

SPLIT_SIZES = (
    NSA_HEADS * HEAD_DIM,
    NSA_KV_HEADS * HEAD_DIM, NSA_KV_HEADS * HEAD_DIM,
    NSA_KV_HEADS * HEAD_DIM, NSA_KV_HEADS * HEAD_DIM,
    NSA_KV_HEADS * HEAD_DIM, NSA_KV_HEADS * HEAD_DIM,
    3 * NSA_HEADS,
    SB_HEADS * HEAD_DIM, SB_HEADS * HEAD_DIM, SB_HEADS * HEAD_DIM,
    SWA_HEADS * HEAD_DIM, SWA_KV_HEADS * HEAD_DIM, SWA_KV_HEADS * HEAD_DIM,
    N_BRANCHES * D_MODEL,
)
D_IN = sum(SPLIT_SIZES)
SPLIT_POINTS = tuple(int(p) for p in np.cumsum(SPLIT_SIZES)[:-1])

kernel_name = "hybrid_nsa_stickbreak_swasink_macaron"


def rms_norm(x, w):
    xf = x.astype(jnp.float32)
    y = xf * lax.rsqrt(jnp.mean(xf * xf, axis=-1, keepdims=True) + NORM_EPS)
    return (y * w.astype(jnp.float32)).astype(x.dtype)


def swiglu(x, w_gate, w_up, w_down):
    return (jax.nn.silu(x @ w_gate) * (x @ w_up)) @ w_down


def t5_bucket(dist):
    max_exact = NUM_BUCKETS // 2
    d = jnp.maximum(dist, 0)
    df = jnp.maximum(d, 1).astype(jnp.float32)
    large = max_exact + (jnp.log(df / max_exact) / math.log(MAX_DISTANCE / max_exact)
                         * (NUM_BUCKETS - max_exact)).astype(jnp.int32)
    large = jnp.minimum(large, NUM_BUCKETS - 1)
    return jnp.where(d < max_exact, d, large)


def masked_softmax(logits, mask):
    p = jax.nn.softmax(jnp.where(mask, logits, NEG_INF), axis=-1)
    return p * mask


def nsa_attention(q, k_cmp, v_cmp, k_slc, v_slc, k_win, v_win, gate_logits,
                  q_norm, k_norm, cmp_pos, cmp_k_w1, cmp_k_w2, cmp_v_w1, cmp_v_w2, bias_table):
    B, S = q.shape[0], q.shape[1]
    G, R, Dh = NSA_KV_HEADS, NSA_HEADS // NSA_KV_HEADS, HEAD_DIM
    scale = Dh ** -0.5
    qg = rms_norm(q, q_norm).reshape(B, S, G, R, Dh).transpose(0, 2, 3, 1, 4)
    gates = jax.nn.sigmoid(gate_logits.astype(jnp.float32)).reshape(B, S, 3, G, R)

    n_cmp = (S - CMP_LEN) // CMP_STRIDE + 1
    cmp_idx = np.arange(n_cmp)[:, None] * CMP_STRIDE + np.arange(CMP_LEN)[None, :]
    cmp_end = jnp.asarray(cmp_idx[:, -1], jnp.int32)

    def compress(t, w1, w2):
        blocks = t[:, cmp_idx] + cmp_pos[:, None, :]
        blocks = blocks.transpose(0, 3, 1, 2, 4).reshape(B, G, n_cmp, CMP_LEN * Dh)
        return jax.nn.silu(blocks @ w1) @ w2

    kc = rms_norm(compress(k_cmp, cmp_k_w1, cmp_k_w2), k_norm)
    vc = compress(v_cmp, cmp_v_w1, cmp_v_w2)

    n_slc = S // SLC_LEN
    slc_start = np.arange(n_slc) * SLC_LEN
    overlap = jnp.asarray((cmp_idx[:, :1] < slc_start[None, :] + SLC_LEN)
                          & (cmp_idx[:, -1:] >= slc_start[None, :]), jnp.float32)
    top_n = min(SLC_TOPN, n_slc)
    ks = rms_norm(k_slc, k_norm).transpose(0, 2, 1, 3).reshape(B, G, n_slc, SLC_LEN, Dh)
    vs = v_slc.transpose(0, 2, 1, 3).reshape(B, G, n_slc, SLC_LEN, Dh)

    pad = ((0, 0), (0, 0), (NSA_WINDOW, 0), (0, 0))
    kw = jnp.pad(rms_norm(k_win, k_norm).transpose(0, 2, 1, 3), pad)
    vw = jnp.pad(v_win.transpose(0, 2, 1, 3), pad)

    head_bias = bias_table[:, :NSA_HEADS].T.reshape(G, R, NUM_BUCKETS).astype(jnp.float32)
    b_idx = jnp.arange(B)[:, None, None, None]
    g_idx = jnp.arange(G)[None, :, None, None]
    g5 = jnp.arange(G)[None, :, None, None, None]
    r5 = jnp.arange(R)[None, None, :, None, None]
    slc_ids = jnp.arange(n_slc)[None, :]

    def block(i):
        q0 = i * Q_BLOCK
        t = q0 + jnp.arange(Q_BLOCK)
        qb = lax.dynamic_slice_in_dim(qg, q0, Q_BLOCK, axis=3)

        dist_c = t[:, None] - cmp_end[None, :]
        s_c = (jnp.einsum('bgrqd,bgnd->bgrqn', qb, kc).astype(jnp.float32) * scale
               + head_bias[:, :, t5_bucket(dist_c)])
        p_c = masked_softmax(s_c, dist_c >= 0)
        o_c = jnp.einsum('bgrqn,bgnd->bgrqd', p_c.astype(vc.dtype), vc)

        imp = jnp.einsum('bgrqn,nj->bgqj', p_c, overlap)
        cur = (t // SLC_LEN)[:, None]
        forced = (slc_ids == 0) | (slc_ids == cur) | (slc_ids == cur - 1)
        score = jnp.where(forced, FORCE_SCORE, jnp.where(slc_ids <= cur, imp, -1.0))
        _, sel = lax.top_k(score, top_n)
        kg = ks[b_idx, g_idx, sel].reshape(B, G, Q_BLOCK, top_n * SLC_LEN, Dh)
        vg = vs[b_idx, g_idx, sel].reshape(B, G, Q_BLOCK, top_n * SLC_LEN, Dh)
        pos = (sel[..., None] * SLC_LEN + jnp.arange(SLC_LEN)).reshape(B, G, Q_BLOCK, top_n * SLC_LEN)
        dist_s = t[None, None, :, None] - pos
        s_s = (jnp.einsum('bgrqd,bgqkd->bgrqk', qb, kg).astype(jnp.float32) * scale
               + head_bias[g5, r5, t5_bucket(dist_s)[:, :, None]])
        p_s = masked_softmax(s_s, (dist_s >= 0)[:, :, None])
        o_s = jnp.einsum('bgrqk,bgqkd->bgrqd', p_s.astype(vg.dtype), vg)

        kb = lax.dynamic_slice_in_dim(kw, q0, Q_BLOCK + NSA_WINDOW, axis=2)
        vb = lax.dynamic_slice_in_dim(vw, q0, Q_BLOCK + NSA_WINDOW, axis=2)
        key_pos = q0 - NSA_WINDOW + jnp.arange(Q_BLOCK + NSA_WINDOW)
        dist_w = t[:, None] - key_pos[None, :]
        mask_w = (dist_w >= 0) & (dist_w < NSA_WINDOW) & (key_pos[None, :] >= 0)
        s_w = (jnp.einsum('bgrqd,bgkd->bgrqk', qb, kb).astype(jnp.float32) * scale
               + head_bias[:, :, t5_bucket(dist_w)])
        p_w = masked_softmax(s_w, mask_w)
        o_w = jnp.einsum('bgrqk,bgkd->bgrqd', p_w.astype(vb.dtype), vb)

        g = lax.dynamic_slice_in_dim(gates, q0, Q_BLOCK, axis=1).transpose(2, 0, 3, 4, 1)[..., None]
        return (g[0] * o_c + g[1] * o_s + g[2] * o_w).astype(q.dtype)

    o = lax.map(block, jnp.arange(S // Q_BLOCK))
    return o.transpose(1, 0, 4, 2, 3, 5).reshape(B, S, NSA_HEADS * Dh)


def stick_breaking_attention(q, k, v):
    B, S = q.shape[0], q.shape[1]
    scale = HEAD_DIM ** -0.5
    qh = q.transpose(0, 2, 1, 3)
    kh = k.transpose(0, 2, 1, 3)
    vh = v.transpose(0, 2, 1, 3)
    key_pos = jnp.arange(S)

    def block(i):
        q0 = i * Q_BLOCK
        t = q0 + jnp.arange(Q_BLOCK)
        qb = lax.dynamic_slice_in_dim(qh, q0, Q_BLOCK, axis=2)
        z = jnp.einsum('bhqd,bhkd->bhqk', qb, kh).astype(jnp.float32) * scale
        mask = key_pos[None, :] < t[:, None]
        log_fail = jnp.where(mask, jax.nn.log_sigmoid(-z), 0.0)
        after = lax.cumsum(log_fail, axis=3, reverse=True) - log_fail
        w = jnp.where(mask, jnp.exp(jax.nn.log_sigmoid(z) + after), 0.0)
        return jnp.einsum('bhqk,bhkd->bhqd', w.astype(vh.dtype), vh)

    o = lax.map(block, jnp.arange(S // Q_BLOCK))
    return o.transpose(1, 0, 3, 2, 4).reshape(B, S, SB_HEADS * HEAD_DIM)


def swa_sink_attention(q, k, v, q_norm, k_norm, sinks, bias_table):
    B, S = q.shape[0], q.shape[1]
    G, R, Dh = SWA_KV_HEADS, SWA_HEADS // SWA_KV_HEADS, HEAD_DIM
    scale = Dh ** -0.5
    qg = rms_norm(q, q_norm).reshape(B, S, G, R, Dh).transpose(0, 2, 3, 1, 4)
    pad = ((0, 0), (0, 0), (SWA_WINDOW, 0), (0, 0))
    kp = jnp.pad(rms_norm(k, k_norm).transpose(0, 2, 1, 3), pad)
    vp = jnp.pad(v.transpose(0, 2, 1, 3), pad)
    head_bias = bias_table[:, NSA_HEADS:].T.reshape(G, R, NUM_BUCKETS).astype(jnp.float32)
    sink = sinks.astype(jnp.float32).reshape(1, G, R, 1, 1)

    def block(i):
        q0 = i * Q_BLOCK
        t = q0 + jnp.arange(Q_BLOCK)
        qb = lax.dynamic_slice_in_dim(qg, q0, Q_BLOCK, axis=3)
        kb = lax.dynamic_slice_in_dim(kp, q0, Q_BLOCK + SWA_WINDOW, axis=2)
        vb = lax.dynamic_slice_in_dim(vp, q0, Q_BLOCK + SWA_WINDOW, axis=2)
        key_pos = q0 - SWA_WINDOW + jnp.arange(Q_BLOCK + SWA_WINDOW)
        dist = t[:, None] - key_pos[None, :]
        mask = (dist >= 0) & (dist < SWA_WINDOW) & (key_pos[None, :] >= 0)
        s = (jnp.einsum('bgrqd,bgkd->bgrqk', qb, kb).astype(jnp.float32) * scale
             + head_bias[:, :, t5_bucket(dist)])
        s = jnp.where(mask, s, NEG_INF)
        m = jnp.maximum(jnp.max(s, axis=-1, keepdims=True), sink)
        e = jnp.exp(s - m)
        p = e / (jnp.sum(e, axis=-1, keepdims=True) + jnp.exp(sink - m))
        return jnp.einsum('bgrqk,bgkd->bgrqd', p.astype(vb.dtype), vb)

    o = lax.map(block, jnp.arange(S // Q_BLOCK))
    return o.transpose(1, 0, 4, 2, 3, 5).reshape(B, S, SWA_HEADS * Dh)


def setup_inputs(seed: int = 0) -> dict:
    key = jax.random.key(seed)
    ks = jax.random.split(key, 32)
    L, D, F = DEPTH, D_MODEL, D_FF

    def w(k, shape, fan_in):
        return jax.random.normal(k, shape, jnp.float32) * fan_in ** -0.5

    def gain(k, shape):
        return 1.0 + 0.02 * jax.random.normal(k, shape, jnp.float32)

    return {
        "x": jax.random.normal(ks[0], (BATCH, SEQ, D), jnp.float32),
        "rel_bias": 0.5 * jax.random.normal(ks[1], (NUM_BUCKETS, N_BIAS_HEADS), jnp.float32),
        "ffn1_norm": gain(ks[2], (L, D)),
        "ffn1_w_gate": w(ks[3], (L, D, F), D),
        "ffn1_w_up": w(ks[4], (L, D, F), D),
        "ffn1_w_down": w(ks[5], (L, F, D), F),
        "mix_norm": gain(ks[6], (L, D)),
        "w_in": w(ks[7], (L, D, D_IN), D),
        "nsa_q_norm": gain(ks[8], (L, HEAD_DIM)),
        "nsa_k_norm": gain(ks[9], (L, HEAD_DIM)),
        "nsa_cmp_pos": 0.5 * jax.random.normal(ks[10], (L, CMP_LEN, HEAD_DIM), jnp.float32),
        "nsa_cmp_k_w1": w(ks[11], (L, CMP_LEN * HEAD_DIM, CMP_HIDDEN), CMP_LEN * HEAD_DIM),
        "nsa_cmp_k_w2": w(ks[12], (L, CMP_HIDDEN, HEAD_DIM), CMP_HIDDEN),
        "nsa_cmp_v_w1": w(ks[13], (L, CMP_LEN * HEAD_DIM, CMP_HIDDEN), CMP_LEN * HEAD_DIM),
        "nsa_cmp_v_w2": w(ks[14], (L, CMP_HIDDEN, HEAD_DIM), CMP_HIDDEN),
        "swa_q_norm": gain(ks[15], (L, HEAD_DIM)),
        "swa_k_norm": gain(ks[16], (L, HEAD_DIM)),
        "swa_sinks": jax.random.normal(ks[17], (L, SWA_HEADS), jnp.float32),
        "w_up_nsa": w(ks[18], (L, NSA_HEADS * HEAD_DIM, D), NSA_HEADS * HEAD_DIM),
        "w_up_sb": w(ks[19], (L, SB_HEADS * HEAD_DIM, D), SB_HEADS * HEAD_DIM),
        "w_up_swa": w(ks[20], (L, SWA_HEADS * HEAD_DIM, D), SWA_HEADS * HEAD_DIM),
        "w_out": w(ks[21], (L, D, D), D),
        "ffn2_norm": gain(ks[22], (L, D)),
        "ffn2_w_gate": w(ks[23], (L, D, F), D),
        "ffn2_w_up": w(ks[24], (L, D, F), D),
        "ffn2_w_down": w(ks[25], (L, F, D), F),
    }


def reference(x, rel_bias, ffn1_norm, ffn1_w_gate, ffn1_w_up, ffn1_w_down, mix_norm, w_in,
              nsa_q_norm, nsa_k_norm, nsa_cmp_pos, nsa_cmp_k_w1, nsa_cmp_k_w2, nsa_cmp_v_w1,
              nsa_cmp_v_w2, swa_q_norm, swa_k_norm, swa_sinks, w_up_nsa, w_up_sb, w_up_swa, w_out,
              ffn2_norm, ffn2_w_gate, ffn2_w_up, ffn2_w_down):
    B, S = x.shape[0], x.shape[1]

    def heads(t, n):
        return t.reshape(B, S, n, HEAD_DIM)

    for l in range(DEPTH):
        h = rms_norm(x, ffn1_norm[l])
        x = x + 0.5 * swiglu(h, ffn1_w_gate[l], ffn1_w_up[l], ffn1_w_down[l])

        h = rms_norm(x, mix_norm[l])
        (nq, nkc, nvc, nks, nvs, nkw, nvw, ngate, sq, sk, sv, wq, wk, wv, bgate) = jnp.split(
            h @ w_in[l], SPLIT_POINTS, axis=-1)
        y_nsa = nsa_attention(heads(nq, NSA_HEADS), heads(nkc, NSA_KV_HEADS), heads(nvc, NSA_KV_HEADS),
                              heads(nks, NSA_KV_HEADS), heads(nvs, NSA_KV_HEADS),
                              heads(nkw, NSA_KV_HEADS), heads(nvw, NSA_KV_HEADS), ngate,
                              nsa_q_norm[l], nsa_k_norm[l], nsa_cmp_pos[l], nsa_cmp_k_w1[l],
                              nsa_cmp_k_w2[l], nsa_cmp_v_w1[l], nsa_cmp_v_w2[l], rel_bias)
        y_sb = stick_breaking_attention(heads(sq, SB_HEADS), heads(sk, SB_HEADS), heads(sv, SB_HEADS))
        y_swa = swa_sink_attention(heads(wq, SWA_HEADS), heads(wk, SWA_KV_HEADS), heads(wv, SWA_KV_HEADS),
                                   swa_q_norm[l], swa_k_norm[l], swa_sinks[l], rel_bias)

        g = jax.nn.sigmoid(bgate.astype(jnp.float32)).reshape(B, S, N_BRANCHES, D_MODEL).astype(x.dtype)
        merged = (g[:, :, 0] * (y_nsa @ w_up_nsa[l]) + g[:, :, 1] * (y_sb @ w_up_sb[l])
                  + g[:, :, 2] * (y_swa @ w_up_swa[l]))
        x = x + merged @ w_out[l]

        h = rms_norm(x, ffn2_norm[l])
        x = x + 0.5 * swiglu(h, ffn2_w_gate[l], ffn2_w_up[l], ffn2_w_down[l])
    return x
```

```python
import contextlib
import math
import numpy as np
import ml_dtypes
import concourse.bass as bass
import concourse.mybir as mybir
from concourse.bass_utils import run_bass_kernel_spmd

F32 = mybir.dt.float32
BF16 = mybir.dt.bfloat16
AF = mybir.ActivationFunctionType
ALU = mybir.AluOpType
AX = mybir.AxisListType

NDMA_SLOTS = 8
EPS = 1e-6
NEG = -30000.0
D = 1024
DFF = 2816
T = 2048
S = 8192
NT = 16
LN8 = math.log(0.125)

O_NQ, O_NKC, O_NVC, O_NKS, O_NVS, O_NKW, O_NVW, O_NG = 0, 512, 640, 768, 896, 1024, 1152, 1280
O_SQ, O_SK, O_SV, O_WQ, O_WK, O_WV, O_BG = 1304, 1560, 1816, 2072, 2328, 2456, 2584


class _Rec:
    def __getattr__(self, name):
        return lambda *a, **k: (name, a, k)


_REC = _Rec()


class Prog:
    ENGS = ["pe", "act", "dve", "pool", "sp"]

    def __init__(self, nc):
        self.nc = nc
        self.ops = {e: [] for e in self.ENGS}
        self.cnt = {e: 0 for e in self.ENGS}
        self.dcnt = {e: 0 for e in self.ENGS}
        self.lastw = {}
        self.readers = {}
        self.waited = {e: {} for e in self.ENGS}
        self.stack = contextlib.ExitStack()
        self.semh = {}

    def sb(self, name, shape, dt):
        return self.stack.enter_context(self.nc.sbuf_tensor("sb_" + name, list(shape), dt))

    def ps(self, name, shape=(128, 512), dt=F32):
        return self.stack.enter_context(self.nc.psum_tensor("ps_" + name, list(shape), dt))

    def sem(self, key):
        if key not in self.semh:
            nm = "s_" + "_".join(str(k) for k in key)
            self.semh[key] = self.stack.enter_context(self.nc.semaphore(nm))
        return self.semh[key]

    def _need(self, eng, r, w):
        need = {}

        def add(sk, v):
            if v > need.get(sk, 0):
                need[sk] = v

        for t in r:
            if t in self.lastw:
                add(*self.lastw[t])
        for t in w:
            if t in self.lastw:
                add(*self.lastw[t])
            for sk, v in self.readers.get(t, ()):
                add(sk, v)
        out = []
        for sk, v in need.items():
            if sk == ("c", "pe") and eng == "pe":
                continue
            if self.waited[eng].get(sk, 0) >= v:
                continue
            self.waited[eng][sk] = v
            out.append((sk, v))
        return out

    def _commit(self, ident, r, w):
        for t in r:
            self.readers.setdefault(t, []).append(ident)
        for t in w:
            self.lastw[t] = ident
            self.readers[t] = []

    def op(self, eng, fn, r=(), w=()):
        waits = self._need(eng, r, w)
        self.cnt[eng] += 1
        ident = (("c", eng), self.cnt[eng])
        self.ops[eng].append((waits, fn(_REC), (("c", eng), 1)))
        self._commit(ident, r, w)

    def dma(self, eng, fn, r=(), w=()):
        k = self.dcnt[eng]
        self.dcnt[eng] += 1
        slot = k % NDMA_SLOTS
        sk = ("d", eng, slot)
        waits = self._need(eng, r, w)
        prev = 16 * (k // NDMA_SLOTS)
        if prev > 0 and self.waited[eng].get(sk, 0) < prev:
            self.waited[eng][sk] = prev
            waits.append((sk, prev))
        ident = (sk, prev + 16)
        self.ops[eng].append((waits, fn(_REC), (sk, 16)))
        self._commit(ident, r, w)

    def finish(self):
        for eng in self.ENGS:
            k = self.dcnt[eng]
            for slot in range(min(k, NDMA_SLOTS)):
                n = (k - 1 - slot) // NDMA_SLOTS + 1
                sk = ("d", eng, slot)
                if self.waited["sp"].get(sk, 0) < 16 * n:
                    self.waited["sp"][sk] = 16 * n
                    self.ops["sp"].append(([(sk, 16 * n)], None, None))
        for eng in ("pe", "act", "dve", "pool"):
            if self.cnt[eng] > 0:
                self.ops["sp"].append(([(("c", eng), self.cnt[eng])], None, None))

    def emit(self):
        nc = self.nc
        for eng in self.ENGS:
            for waits, fn, inc in self.ops[eng]:
                for sk, v in waits:
                    self.sem(sk)
                if inc is not None:
                    self.sem(inc[0])
        P = self

        def run(engname):
            def body(e):
                for waits, fn, inc in P.ops[engname]:
                    for sk, v in waits:
                        e.wait_ge(P.semh[sk], v)
                    if fn is not None:
                        ins = getattr(e, fn[0])(*fn[1], **fn[2])
                        ins.then_inc(P.semh[inc[0]], inc[1])
            return body

        with nc.Block() as block:
            block.tensor(run("pe"))
            block.scalar(run("act"))
            block.vector(run("dve"))
            block.gpsimd(run("pool"))
            block.sync(run("sp"))
        self.stack.close()


class NS:
    pass


def _sl(i, n):
    return slice(i * n, (i + 1) * n)


def setup_common(P, C):
    C.ones = P.sb("ones", [128, 128], BF16)
    C.bd = P.sb("bd", [128, 128], BF16)
    P.op("dve", lambda e: e.memset(C.ones[:], 1.0), w=["ones"])
    P.op("dve", lambda e: e.memset(C.bd[:], 0.0), w=["bd"])
    P.op("dve", lambda e: e.memset(C.bd[0:64, 0:64], 1.0), r=["bd"], w=["bd"])
    P.op("dve", lambda e: e.memset(C.bd[64:128, 64:128], 1.0), r=["bd"], w=["bd"])
    C.pb = [P.ps("pb%d" % i) for i in range(8)]
    C.sq = P.sb("sq", [128, 8, 512], BF16)
    C.lnt = P.sb("lnt", [128, 512], F32)
    C.rstd = P.sb("rstd", [128, 512], F32)
    C.wgb = [P.sb("wgb%d" % i, [128, 8, 128], BF16) for i in range(2)]
    C.wub = [P.sb("wub%d" % i, [128, 8, 128], BF16) for i in range(2)]
    C.wdb = [P.sb("wdb%d" % i, [128, 11, 128], BF16) for i in range(2)]
    C.sg = [P.sb("sg%d" % i, [128, 512], BF16) for i in range(2)]


def rmsnorm_h(P, C, x, h, gain, tbs=range(4)):
    for tb in tbs:
        ts = _sl(tb, 512)
        for c in range(8):
            P.op("act", lambda e, c=c, ts=ts: e.activation(C.sq[:, c, :], x[:, c, ts], AF.Square),
                 r=[("x", c, tb)], w=[("sq", c)])
        for c in range(8):
            P.op("pe", lambda e, c=c: e.matmul(C.pb[6][:], lhsT=C.ones[:], rhs=C.sq[:, c, :],
                                               start=(c == 0), stop=(c == 7)),
                 r=[("sq", c), "ones"], w=[("pb", 6)])
        P.op("act", lambda e: e.activation(C.lnt[:], C.pb[6][:], AF.Ln, bias=EPS, scale=1.0 / D),
             r=[("pb", 6)], w=["lnt"])
        P.op("act", lambda e: e.activation(C.rstd[:], C.lnt[:], AF.Exp, scale=-0.5), r=["lnt"], w=["rstd"])
        for c in range(8):
            P.op("dve", lambda e, c=c, ts=ts: e.scalar_tensor_tensor(
                out=h[:, c, ts], in0=x[:, c, ts], scalar=gain[:, c:c + 1], in1=C.rstd[:],
                op0=ALU.mult, op1=ALU.mult),
                 r=[("x", c, tb), "rstd", "gains"], w=[("h", c, tb)])


def ffn(P, C, x, h, act, wg_d, wu_d, wd_d):
    k = 0
    k2 = 0
    for half in range(2):
        for fl in range(11):
            f = half * 11 + fl
            s = f % 2
            P.dma("pool", lambda e, s=s, f=f: e.dma_start(out=C.wgb[s][:].rearrange("p c j -> p (c j)"), in_=wg_d[f]),
                  w=[("wg", s)])
            P.dma("pool", lambda e, s=s, f=f: e.dma_start(out=C.wub[s][:].rearrange("p c j -> p (c j)"), in_=wu_d[f]),
                  w=[("wu", s)])
            for tb in range(4):
                ts = _sl(tb, 512)
                pg = k % 2
                pu = 2 + k % 2
                sgi = k % 2
                k += 1
                for c in range(8):
                    P.op("pe", lambda e, c=c, s=s, ts=ts, pg=pg: e.matmul(
                        C.pb[pg][:], lhsT=C.wgb[s][:, c, :], rhs=h[:, c, ts], start=(c == 0), stop=(c == 7)),
                         r=[("wg", s), ("h", c, tb)], w=[("pb", pg)])
                for c in range(8):
                    P.op("pe", lambda e, c=c, s=s, ts=ts, pu=pu: e.matmul(
                        C.pb[pu][:], lhsT=C.wub[s][:, c, :], rhs=h[:, c, ts], start=(c == 0), stop=(c == 7)),
                         r=[("wu", s), ("h", c, tb)], w=[("pb", pu)])
                P.op("act", lambda e, pg=pg, sgi=sgi: e.activation(C.sg[sgi][:], C.pb[pg][:], AF.Silu),
                     r=[("pb", pg)], w=[("sg", sgi)])
                P.op("dve", lambda e, pu=pu, sgi=sgi, fl=fl, ts=ts: e.tensor_tensor(
                    out=act[:, fl, ts], in0=C.sg[sgi][:], in1=C.pb[pu][:], op=ALU.mult),
                     r=[("sg", sgi), ("pb", pu)], w=[("act", fl, tb)])
        for dc in range(8):
            s = dc % 2
            P.dma("pool", lambda e, s=s, half=half, dc=dc: e.dma_start(
                out=C.wdb[s][:].rearrange("p f j -> p (f j)"), in_=wd_d[half, dc]), w=[("wd", s)])
            for tb in range(4):
                ts = _sl(tb, 512)
                po = 4 + k2 % 2
                k2 += 1
                for fl in range(11):
                    P.op("pe", lambda e, fl=fl, s=s, ts=ts, po=po: e.matmul(
                        C.pb[po][:], lhsT=C.wdb[s][:, fl, :], rhs=act[:, fl, ts], start=(fl == 0), stop=(fl == 10)),
                         r=[("wd", s), ("act", fl, tb)], w=[("pb", po)])
                P.op("dve", lambda e, dc=dc, ts=ts, po=po: e.scalar_tensor_tensor(
                    out=x[:, dc, ts], in0=C.pb[po][:], scalar=0.5, in1=x[:, dc, ts], op0=ALU.mult, op1=ALU.add),
                     r=[("pb", po), ("x", dc, tb)], w=[("x", dc, tb)])


FM_KINDS = ([("n", 0, True)] * 4 + [("p", 1.0), ("p", 1.0), ("n", 1, False), ("n", 1, False)]
            + [("p", 0.125)] * 2 + [("p", 1.0)] * 2 + [("n", 2, True)] * 2 + [("n", 3, False)])


def build_A():
    nc = bass.Bass("TRN2", target_bir_lowering=False)
    dt = lambda n, s, t=F32, k="ExternalInput": nc.dram_tensor(n, s, t, kind=k).ap()
    xT_d = dt("xT", [D, T])
    n1_d = dt("n1", [128, 8])
    nm_d = dt("nm", [128, 8])
    wg_d = dt("wg", [22, 128, 1024])
    wu_d = dt("wu", [22, 128, 1024])
    wd_d = dt("wd", [2, 8, 128, 11 * 128])
    wfm_d = dt("wfm", [15, 128, 1024])
    wtm_d = dt("wtm", [8, 128, 664])
    gfm_d = dt("gfm", [128, 4])
    x1T_d = dt("x1T", [D, T], F32, "ExternalOutput")
    fm_d = dt("fm", [15, 128, T], BF16, "ExternalOutput")
    tm_d = dt("tm", [T, 640], BF16, "ExternalOutput")
    tmg_d = dt("tmg", [T, 24], F32, "ExternalOutput")

    P = Prog(nc)
    C = NS()
    setup_common(P, C)
    x = P.sb("x", [128, 8, T], F32)
    h = P.sb("h", [128, 8, T], BF16)
    act = P.sb("act", [128, 11, T], BF16)
    gn = P.sb("gn", [128, 20], F32)
    for c in range(8):
        P.dma("sp", lambda e, c=c: e.dma_start(out=x[:, c, :], in_=xT_d[_sl(c, 128), :]),
              w=[("x", c, tb) for tb in range(4)])
    P.dma("sp", lambda e: e.dma_start(out=gn[:, 0:8], in_=n1_d), w=["gains"])
    P.dma("sp", lambda e: e.dma_start(out=gn[:, 8:16], in_=nm_d), r=["gains"], w=["gains"])
    P.dma("sp", lambda e: e.dma_start(out=gn[:, 16:20], in_=gfm_d), r=["gains"], w=["gains"])

    rmsnorm_h(P, C, x, h, gn[:, 0:8])
    ffn(P, C, x, h, act, wg_d, wu_d, wd_d)
    rmsnorm_h(P, C, x, h, gn[:, 8:16])

    wtm = P.sb("wtm", [128, 8, 664], BF16)
    for c in range(8):
        P.dma("pool", lambda e, c=c: e.dma_start(out=wtm[:, c, :], in_=wtm_d[c]), w=[("wtm", c)])
    ofm = [P.sb("ofm%d" % i, [128, 512], BF16) for i in range(2)]
    sqb = P.sb("sqb", [128, 512], BF16)
    k = 0
    for ch in range(15):
        s = ch % 2
        kind = FM_KINDS[ch]
        P.dma("pool", lambda e, s=s, ch=ch: e.dma_start(out=C.wgb[s][:].rearrange("p c j -> p (c j)"), in_=wfm_d[ch]),
              w=[("wg", s)])
        for tb in range(4):
            ts = _sl(tb, 512)
            pz = k % 2
            pn = 2 + k % 2
            oi = k % 2
            k += 1
            for c in range(8):
                P.op("pe", lambda e, c=c, s=s, ts=ts, pz=pz: e.matmul(
                    C.pb[pz][:], lhsT=C.wgb[s][:, c, :], rhs=h[:, c, ts], start=(c == 0), stop=(c == 7)),
                     r=[("wg", s), ("h", c, tb)], w=[("pb", pz)])
            if kind[0] == "n":
                P.op("act", lambda e, pz=pz: e.activation(sqb[:], C.pb[pz][:], AF.Square), r=[("pb", pz)], w=["sqb"])
                P.op("pe", lambda e, pn=pn: e.matmul(C.pb[pn][:], lhsT=C.bd[:], rhs=sqb[:], start=True, stop=True),
                     r=["sqb", "bd"], w=[("pb", pn)])
                P.op("act", lambda e, pn=pn: e.activation(C.lnt[:], C.pb[pn][:], AF.Ln, bias=EPS, scale=1.0 / 64),
                     r=[("pb", pn)], w=["lnt"])
                bias = LN8 if kind[2] else 0.0
                P.op("act", lambda e, bias=bias: e.activation(C.rstd[:], C.lnt[:], AF.Exp, scale=-0.5, bias=bias),
                     r=["lnt"], w=["rstd"])
                gc = 16 + kind[1]
                P.op("dve", lambda e, pz=pz, oi=oi, gc=gc: e.scalar_tensor_tensor(
                    out=ofm[oi][:], in0=C.pb[pz][:], scalar=gn[:, gc:gc + 1], in1=C.rstd[:],
                    op0=ALU.mult, op1=ALU.mult), r=[("pb", pz), "rstd", "gains"], w=[("ofm", oi)])
            else:
                P.op("act", lambda e, pz=pz, oi=oi, sc=kind[1]: e.mul(ofm[oi][:], C.pb[pz][:], sc),
                     r=[("pb", pz)], w=[("ofm", oi)])
            P.dma("sp", lambda e, ch=ch, ts=ts, oi=oi: e.dma_start(out=fm_d[ch, :, ts], in_=ofm[oi][:]),
                  r=[("ofm", oi)])
    otm = [P.sb("otm%d" % i, [128, 640], BF16) for i in range(2)]
    otg = [P.sb("otg%d" % i, [128, 24], F32) for i in range(2)]
    for tt in range(16):
        tsl = _sl(tt, 128)
        tb = tt // 4
        s = tt % 2
        pa = 4 + s
        pbn = 6 + s
        for c in range(8):
            P.op("pe", lambda e, c=c, tsl=tsl, pa=pa: e.matmul(
                C.pb[pa][:, 0:512], lhsT=h[:, c, tsl], rhs=wtm[:, c, 0:512], start=(c == 0), stop=(c == 7)),
                 r=[("wtm", c), ("h", c, tb)], w=[("pb", pa)])
        for c in range(8):
            P.op("pe", lambda e, c=c, tsl=tsl, pbn=pbn: e.matmul(
                C.pb[pbn][:, 0:152], lhsT=h[:, c, tsl], rhs=wtm[:, c, 512:664], start=(c == 0), stop=(c == 7)),
                 r=[("wtm", c), ("h", c, tb)], w=[("pb", pbn)])
        P.op("act", lambda e, s=s, pa=pa: e.copy(otm[s][:, 0:512], C.pb[pa][:, 0:512]),
             r=[("pb", pa)], w=[("otm", s, 0)])
        P.op("dve", lambda e, s=s, pbn=pbn: e.tensor_copy(otm[s][:, 512:640], C.pb[pbn][:, 0:128]),
             r=[("pb", pbn)], w=[("otm", s, 1)])
        P.op("dve", lambda e, s=s, pbn=pbn: e.tensor_copy(otg[s][:], C.pb[pbn][:, 128:152]),
             r=[("pb", pbn)], w=[("otg", s)])
        P.dma("sp", lambda e, s=s, tsl=tsl: e.dma_start(out=tm_d[tsl, :], in_=otm[s][:]),
              r=[("otm", s, 0), ("otm", s, 1)])
        P.dma("sp", lambda e, s=s, tsl=tsl: e.dma_start(out=tmg_d[tsl, :], in_=otg[s][:]), r=[("otg", s)])
    for c in range(8):
        P.dma("sp", lambda e, c=c: e.dma_start(out=x1T_d[_sl(c, 128), :], in_=x[:, c, :]),
              r=[("x", c, tb) for tb in range(4)])
    P.finish()
    P.emit()
    return nc


def fm_cols():
    cols = []
    for r in range(4):
        cols += [O_NQ + (0 * 4 + r) * 64 + d for d in range(64)] + [O_NQ + (1 * 4 + r) * 64 + d for d in range(64)]
    cols += list(range(O_NKC, O_NKC + 128)) + list(range(O_NVC, O_NVC + 128))
    cols += list(range(O_NKS, O_NKS + 128)) + list(range(O_NKW, O_NKW + 128))
    cols += list(range(O_SQ, O_SQ + 256)) + list(range(O_SK, O_SK + 256))
    for r in range(2):
        cols += [O_WQ + (0 * 2 + r) * 64 + d for d in range(64)] + [O_WQ + (1 * 2 + r) * 64 + d for d in range(64)]
    cols += list(range(O_WK, O_WK + 128))
    return np.array(cols)


def tm_cols():
    return np.array(list(range(O_NVS, O_NVS + 128)) + list(range(O_NVW, O_NVW + 128))
                    + list(range(O_SV, O_SV + 256)) + list(range(O_WV, O_WV + 128)) + list(range(O_NG, O_NG + 24)))


def kmajor(w, nk):
    return np.ascontiguousarray(w.reshape(nk, 128, -1).transpose(1, 0, 2))


def prep_ffn(wg, wu, wd):
    wg_a = np.ascontiguousarray(wg.reshape(8, 128, 22, 128).transpose(2, 1, 0, 3)).reshape(22, 128, 1024)
    wu_a = np.ascontiguousarray(wu.reshape(8, 128, 22, 128).transpose(2, 1, 0, 3)).reshape(22, 128, 1024)
    wd_a = np.ascontiguousarray(wd.reshape(2, 11, 128, 8, 128).transpose(0, 3, 2, 1, 4)).reshape(2, 8, 128, 11 * 128)
    return wg_a, wu_a, wd_a


def vec8(v):
    return np.ascontiguousarray(v.reshape(8, 128).T)


def prep_A(inp, l):
    w_in = inp["w_in"][l]
    wg_a, wu_a, wd_a = prep_ffn(inp["ffn1_w_gate"][l], inp["ffn1_w_up"][l], inp["ffn1_w_down"][l])
    wfm = w_in[:, fm_cols()]
    wfm_a = np.ascontiguousarray(wfm.reshape(8, 128, 15, 128).transpose(2, 1, 0, 3)).reshape(15, 128, 1024)
    wtm = w_in[:, tm_cols()]
    wtm_a = np.ascontiguousarray(wtm.reshape(8, 128, 664))
    g2 = lambda v: np.concatenate([v, v])
    gfm = np.ascontiguousarray(np.stack([g2(inp["nsa_q_norm"][l]), g2(inp["nsa_k_norm"][l]),
                                         g2(inp["swa_q_norm"][l]), g2(inp["swa_k_norm"][l])], axis=1))
    return dict(n1=vec8(inp["ffn1_norm"][l]), nm=vec8(inp["mix_norm"][l]), wg=wg_a, wu=wu_a, wd=wd_a,
                wfm=wfm_a, wtm=wtm_a, gfm=gfm)


def shard_xT(x):
    out = []
    for b in range(2):
        xb = x[b].reshape(16, 4, 128, D)
        for r in range(4):
            out.append(np.ascontiguousarray(xb[:, r].reshape(T, D).T))
    return out


def unshard_xT(parts):
    x = np.empty((2, 16, 4, 128, D), np.float32)
    for b in range(2):
        for r in range(4):
            x[b, :, r] = parts[b * 4 + r].T.reshape(16, 128, D)
    return x.reshape(2, S, D)


def _barrier(P):
    for eng in P.ENGS:
        waits = []
        for e2 in ("pe", "act", "dve", "pool"):
            if e2 != eng and P.cnt[e2] > 0:
                sk, v = ("c", e2), P.cnt[e2]
                if P.waited[eng].get(sk, 0) < v:
                    P.waited[eng][sk] = v
                    waits.append((sk, v))
        for q in P.ENGS:
            k = P.dcnt[q]
            for slot in range(min(k, NDMA_SLOTS)):
                n = (k - 1 - slot) // NDMA_SLOTS + 1
                sk = ("d", q, slot)
                if P.waited[eng].get(sk, 0) < 16 * n:
                    P.waited[eng][sk] = 16 * n
                    waits.append((sk, 16 * n))
        if waits:
            P.ops[eng].append((waits, None, None))


class Arena:
    def __init__(self, P, nbytes):
        self.P = P
        self.cap = nbytes
        self.t = P.sb("arena", [128, nbytes // 2], BF16)
        self.off = 0
        self.marks = []

    def alloc(self, shape, dt=BF16):
        n = int(np.prod(shape[1:]))
        esz = 4 if dt == F32 else 2
        nb = (n * esz + 63) // 64 * 64
        assert self.off + nb <= self.cap, ("arena overflow", self.off, nb)
        v = self.t[0:shape[0], self.off // 2:(self.off + nb) // 2]
        self.off += nb
        if dt == F32:
            v = v.bitcast(F32)
        v = v[:, 0:n]
        if len(shape) > 2:
            names = "abcde"[:len(shape) - 1]
            v = v.rearrange("p (" + " ".join(names) + ") -> p " + " ".join(names),
                            **{k: int(s) for k, s in zip(names, shape[1:])})
        return v

    def push(self):
        self.marks.append(self.off)

    def pop(self):
        _barrier(self.P)
        self.off = self.marks.pop()


def phase_sb(P, C, AR, d):
    al = AR.alloc
    pb = C.pb
    AR.push()
    KS = al([128, 8192])
    VS = al([128, 64, 128])
    QS = al([128, 2048])
    msb = al([128, 4, 128])
    et = [al([128, 512], F32) for _ in range(2)]
    spb = [al([128, 512]) for _ in range(2)]
    wb = [al([128, 512]) for _ in range(2)]
    cbf = [al([1, 512]) for _ in range(2)]
    P.dma("sp", lambda e: e.dma_start(out=msb, in_=d.msb), w=["msb"])
    for cc in range(2):
        P.dma("sp", lambda e, cc=cc: e.dma_start(out=KS, in_=d.kt[3 + cc]), w=["KS"])
        P.dma("sp", lambda e, cc=cc: e.dma_start(out=VS, in_=d.vsb[cc]), w=["VS"])
        P.dma("sp", lambda e, cc=cc: e.dma_start(out=QS, in_=d.fmq[4 + cc]), w=["QS"])
        for sbk in range(4):
            nun = 16 * sbk + 16
            q0 = sbk * 512

            def info(u):
                kb = nun - 1 - u
                rel = kb - 16 * sbk
                if rel >= 0:
                    return kb, rel // 4, rel % 4, (rel % 4 == 3)
                return kb, 0, None, False

            def qk(s, u):
                kb, a, j, join = info(u)
                z = pb[s * 2 + u % 2]
                hp = slice(s * 64, s * 64 + 64)
                P.op("pe", lambda e: e.matmul(z[:, a * 128:512], lhsT=KS[hp, kb * 128:(kb + 1) * 128],
                                              rhs=QS[hp, q0 + a * 128:q0 + 512], start=True, stop=False),
                     r=["KS", "QS"], w=[("zb", s, u % 2)])

            for s in range(2):
                qk(s, 0)
            for u in range(nun):
                kb, a, j, join = info(u)
                c0 = a * 128
                cols = slice(c0, 512)
                newc = slice(c0, c0 + 128)
                oldc = slice(c0 + 128, 512) if join else cols
                has_old = (oldc.start < 512)
                for s in range(2):
                    z = pb[s * 2 + u % 2]
                    zt = ("zb", s, u % 2)
                    P.op("act", lambda e, z=z, s=s: e.activation(et[s][:, cols], z[:, cols], AF.Exp),
                         r=[zt], w=[("et", s)])
                    P.op("act", lambda e, s=s: e.activation(spb[s][:, cols], et[s][:, cols], AF.Ln, bias=1.0),
                         r=[("et", s)], w=[("spb", s)])
                    if j is not None:
                        P.op("dve", lambda e, s=s: e.tensor_tensor(out=spb[s][:, newc], in0=spb[s][:, newc],
                                                                   in1=msb[:, j, :], op=ALU.mult),
                             r=[("spb", s), "msb"], w=[("spb", s)])
                    P.op("pe", lambda e, z=z, s=s: e.matmul(z[:, cols], lhsT=C.nu, rhs=spb[s][:, cols],
                                                            start=False, stop=(not has_old)),
                         r=[("spb", s), "nu"], w=[zt])
                    if has_old:
                        P.op("pe", lambda e, z=z, s=s: e.matmul(z[:, oldc], lhsT=C.negone, rhs=cbf[s][0:1, oldc],
                                                                start=False, stop=True),
                             r=[("cbf", s), "negone"], w=[zt])
                    cb = pb[4 + s]
                    P.op("pe", lambda e, cb=cb, s=s: e.matmul(cb[0:1, cols], lhsT=C.ones[:, 0:1], rhs=spb[s][:, cols],
                                                              start=(u == 0), stop=True),
                         r=[("spb", s), "ones"], w=[("cb", s)])
                    if u + 1 < nun:
                        P.op("dve", lambda e, cb=cb, s=s: e.tensor_copy(cbf[s][0:1, cols], cb[0:1, cols]),
                             r=[("cb", s)], w=[("cbf", s)])
                        qk(s, u + 1)
                for s in range(2):
                    z = pb[s * 2 + u % 2]
                    zt = ("zb", s, u % 2)
                    P.op("act", lambda e, z=z, s=s: e.activation(wb[s][:, cols], z[:, cols], AF.Exp),
                         r=[zt], w=[("wb", s)])
                    if j is not None:
                        P.op("dve", lambda e, s=s: e.tensor_tensor(out=wb[s][:, newc], in0=wb[s][:, newc],
                                                                   in1=msb[:, j, :], op=ALU.mult),
                             r=[("wb", s), "msb"], w=[("wb", s)])
                    ob = pb[6]
                    hp = slice(s * 64, s * 64 + 64)
                    last = (kb == 0)
                    P.op("pe", lambda e, s=s, hp=hp: e.matmul(ob[hp, cols], lhsT=VS[:, kb, hp], rhs=wb[s][:, cols],
                                                              start=(u == 0), stop=last),
                         r=[("wb", s), "VS"], w=[("ob", s)])
            P.op("act", lambda e, cc=cc, q0=q0: e.copy(C.ysbT[:, cc, q0:q0 + 512], pb[6][:, :]),
                 r=[("ob", 0), ("ob", 1)], w=[("ysb", cc, sbk)])
    AR.pop()


def phase_swa(P, C, AR, d):
    al = AR.alloc
    pb = C.pb
    AR.push()
    KW = al([128, 16, 256])
    VW = al([128, 16, 2, 2, 65])
    QW = al([128, 2, 2048])
    BSW = al([128, 2, 2, 256], F32)
    esink = al([128, 4], F32)
    tmp = [al([128, 256], F32) for _ in range(2)]
    pw = [al([128, 256]) for _ in range(2)]
    den = al([128, 4], F32)
    rden = al([128, 4], F32)
    ytm = [al([128, 256]) for _ in range(2)]
    P.dma("sp", lambda e: e.dma_start(out=KW, in_=d.kw2), w=["KW"])
    P.dma("sp", lambda e: e.dma_start(out=VW.rearrange("p a b c d -> p (a b c d)"), in_=d.vw2), w=["VW"])
    for c in range(2):
        P.dma("sp", lambda e, c=c: e.dma_start(out=QW[:, c, :], in_=d.fmq[6 + c]), w=[("QW", c)])
    P.dma("sp", lambda e: e.dma_start(out=BSW.rearrange("p a b c -> p (a b c)"), in_=d.bsw), w=["BSW"])
    P.dma("sp", lambda e: e.dma_start(out=esink, in_=d.sinks), w=["esink"])
    P.op("act", lambda e: e.activation(esink, esink, AF.Exp), r=["esink"], w=["esink"])
    k = 0
    ob = pb[4]
    for i in range(16):
        qs = _sl(i, 128)
        for g in range(2):
            gp = slice(g * 64, g * 64 + 64)
            for di in range(2):
                s = k % 2
                k += 1
                z = pb[s]
                P.op("pe", lambda e, z=z, gp=gp, di=di: e.matmul(z[:, 0:256], lhsT=KW[gp, i, di * 128:(di + 1) * 128],
                                                                 rhs=QW[gp, :, qs], start=True, stop=True),
                     r=["KW", ("QW", 0), ("QW", 1)], w=[("z", s)])
                P.op("dve", lambda e, z=z, s=s, di=di, g=g: e.tensor_tensor(out=tmp[s], in0=z[:, 0:256],
                                                                            in1=BSW[:, di, g, :], op=ALU.add),
                     r=[("z", s), "BSW"], w=[("tmp", s)])
                P.op("act", lambda e, s=s: e.activation(pw[s], tmp[s], AF.Exp), r=[("tmp", s)], w=[("pw", s)])
                for r in range(2):
                    hcol = (g * 2 + r) * 65
                    P.op("pe", lambda e, s=s, r=r, hcol=hcol, di=di, g=g: e.matmul(
                        ob[:, hcol:hcol + 65], lhsT=pw[s][:, r * 128:(r + 1) * 128], rhs=VW[:, i, di, g, :],
                        start=(di == 0 and r == 0), stop=(di == 1)), r=[("pw", s), "VW"], w=["ob_w"])
        P.op("dve", lambda e: e.tensor_tensor(out=den, in0=ob[:, 64:260:65], in1=esink, op=ALU.add),
             r=["ob_w", "esink"], w=["den"])
        P.op("dve", lambda e: e.reciprocal(rden, den), r=["den"], w=["rden"])
        ys = i % 2
        for h in range(4):
            P.op("dve", lambda e, h=h, ys=ys: e.tensor_scalar(ytm[ys][:, h * 64:(h + 1) * 64], ob[:, h * 65:h * 65 + 64],
                                                              rden[:, h:h + 1], None, op0=ALU.mult),
                 r=["ob_w", "rden"], w=[("ytm", ys)])
        for c in range(2):
            P.op("pe", lambda e, c=c, ys=ys: e.transpose(C.ptb[:, c * 128:(c + 1) * 128], ytm[ys][:, c * 128:(c + 1) * 128],
                                                          C.ident), r=[("ytm", ys), "ident"], w=["ptb"])
        P.op("act", lambda e, qs=qs: e.copy(C.yswT[:, :, qs], C.ptb[:, 0:256].rearrange("p (c q) -> p c q", c=2)),
             r=["ptb"], w=[("ysw", i)])
    AR.pop()


def phase_nsa(P, C, AR, d):
    al = AR.alloc
    pb = C.pb
    AR.push()
    KC = al([128, 512])
    RC = al([128, 4, 2, 193])
    AR.push()
    CK = al([128, 8192])
    CV = al([128, 8192])
    W1 = [al([128, 32, 256]) for _ in range(2)]
    W2 = [al([128, 2, 64]) for _ in range(2)]
    POST = al([128, 32])
    hid = al([128, 2, 512])
    cpos = al([128, 4], F32)
    sqc = al([128, 512])
    lnt = al([128, 512], F32)
    rst = al([128, 512], F32)
    gk = al([128, 1], F32)
    P.dma("sp", lambda e: e.dma_start(out=CK, in_=d.kt[0]), w=["CK"])
    P.dma("sp", lambda e: e.dma_start(out=CV, in_=d.kt[1]), w=["CV"])
    P.dma("sp", lambda e: e.dma_start(out=gk, in_=d.gk), w=["gk"])
    for wi in range(2):
        P.dma("pool", lambda e, wi=wi: e.dma_start(out=W1[wi].rearrange("p j h -> p (j h)"), in_=d.w1[wi]), w=[("W1", wi)])
        P.dma("pool", lambda e, wi=wi: e.dma_start(out=W2[wi].rearrange("p c h -> p (c h)"), in_=d.w2[wi]), w=[("W2", wi)])
    P.dma("pool", lambda e: e.dma_start(out=POST, in_=d.post), w=["POST"])
    for g in range(2):
        P.dma("sp", lambda e, g=g: e.dma_start(out=RC[:, :, g, 64:193], in_=d.ovl), w=[("RCo", g)])
    P.op("dve", lambda e: e.memset(hid[:, :, 511:512], 0.0), w=["hidpad"])
    kk = 0
    SRC = [CK, CV]
    for wi in range(2):
        for hc in range(2):
            for j in range(32):
                P.op("pe", lambda e, wi=wi, hc=hc, j=j: e.matmul(pb[5][:, 0:1], lhsT=W1[wi][0:64, j, hc * 128:(hc + 1) * 128],
                                                                 rhs=POST[0:64, j:j + 1], start=(j == 0), stop=(j == 31)),
                     r=[("W1", wi), "POST"], w=["pcp"])
            P.op("act", lambda e, wi=wi, hc=hc: e.copy(cpos[:, wi * 2 + hc:wi * 2 + hc + 1], pb[5][:, 0:1]),
                 r=["pcp"], w=[("cpos", wi, hc)])
        for g in range(2):
            gp = slice(g * 64, g * 64 + 64)
            for hc in range(2):
                z = pb[kk % 2]
                zt = ("z", kk % 2)
                kk += 1
                for j in range(32):
                    P.op("pe", lambda e, z=z, wi=wi, hc=hc, j=j, gp=gp: e.matmul(
                        z[:, 0:511], lhsT=W1[wi][gp, j, hc * 128:(hc + 1) * 128], rhs=SRC[wi][gp, j:j + 16 * 510 + 1:16],
                        start=(j == 0), stop=(j == 31)), r=[("W1", wi), ["CK", "CV"][wi]], w=[zt])
                P.op("act", lambda e, z=z, wi=wi, hc=hc: e.activation(hid[:, hc, 0:511], z[:, 0:511], AF.Silu,
                                                                       bias=cpos[:, wi * 2 + hc:wi * 2 + hc + 1]),
                     r=[zt, ("cpos", wi, hc), "hidpad"], w=[("hid", hc)])
            if wi == 0:
                for hc in range(2):
                    P.op("pe", lambda e, hc=hc, gp=gp: e.matmul(pb[2][gp, 0:512], lhsT=W2[0][:, hc, :], rhs=hid[:, hc, :],
                                                                start=(hc == 0), stop=(hc == 1)),
                         r=[("hid", hc), ("W2", 0)], w=[("kcp", g)])
            else:
                for ct in range(4):
                    for hc in range(2):
                        P.op("pe", lambda e, hc=hc, ct=ct: e.matmul(pb[3][:, ct * 64:(ct + 1) * 64],
                                                                    lhsT=hid[:, hc, ct * 128:(ct + 1) * 128], rhs=W2[1][:, hc, :],
                                                                    start=(hc == 0), stop=(hc == 1)),
                             r=[("hid", hc), ("W2", 1)], w=["vcp"])
                P.op("dve", lambda e, g=g: e.tensor_copy(RC[:, :, g, 0:64], pb[3][:, 0:256].rearrange("p (c d) -> p c d", c=4)),
                     r=["vcp"], w=[("RCv", g)])
        if wi == 0:
            P.op("act", lambda e: e.activation(sqc, pb[2][:, :], AF.Square), r=[("kcp", 0), ("kcp", 1)], w=["sqc"])
            P.op("pe", lambda e: e.matmul(pb[3][:, :], lhsT=C.bd, rhs=sqc, start=True, stop=True), r=["sqc", "bd"], w=["vcp"])
            P.op("act", lambda e: e.activation(lnt, pb[3][:, :], AF.Ln, bias=EPS, scale=1.0 / 64), r=["vcp"], w=["lntc"])
            P.op("act", lambda e: e.activation(rst, lnt, AF.Exp, scale=-0.5), r=["lntc"], w=["rstc"])
            P.op("dve", lambda e: e.scalar_tensor_tensor(out=KC, in0=pb[2][:, :], scalar=gk[:, 0:1], in1=rst,
                                                         op0=ALU.mult, op1=ALU.mult),
                 r=[("kcp", 0), ("kcp", 1), "rstc", "gk"], w=["KC"])
    AR.pop()
    QN = al([128, 4, 2048])
    KSL = al([128, 8192])
    VSL = al([128, 64, 2, 65])
    FM = al([128, 8192])
    BS = al([128, 2, 5, 512], F32)
    BW = al([128, 2, 3, 512], F32)
    BC = al([32, 2, 512], F32)
    SEL = al([32, 16, 2, 128], F32)
    AT = al([128, 248], F32)
    CT = al([128, 248], F32)
    b31 = al([128, 8], F32)
    gsig = al([128, 16, 24], F32)
    kwn = [al([128, 640]) for _ in range(2)]
    vwn = [al([128, 5, 2, 65]) for _ in range(2)]
    ex = [al([128, 512]) for _ in range(2)]
    tmpf = [al([128, 512], F32) for _ in range(2)]
    yacc = al([128, 512], F32)
    ybf = al([128, 512])
    m01 = al([128, 128])
    m4 = [al([128, 4, 128]) for _ in range(2)]
    score = al([128, 128], F32)
    scr2 = al([128, 128], F32)
    mx = al([128, 16], F32)
    zr = al([128, 4], F32)
    rz = al([128, 4], F32)
    coef = al([128, 4], F32)
    for c in range(4):
        P.dma("sp", lambda e, c=c: e.dma_start(out=QN[:, c, :], in_=d.fmq[c]), w=[("QN", c)])
    P.dma("sp", lambda e: e.dma_start(out=KSL, in_=d.kt[2]), w=["KSL"])
    P.dma("sp", lambda e: e.dma_start(out=VSL.rearrange("p t g c -> p (t g c)"), in_=d.vsl), w=["VSL"])
    P.dma("sp", lambda e: e.dma_start(out=FM, in_=d.fmask), w=["FM"])
    P.dma("sp", lambda e: e.dma_start(out=BS.rearrange("p a b c -> p (a b c)"), in_=d.bs), w=["BS"])
    P.dma("sp", lambda e: e.dma_start(out=BW.rearrange("p a b c -> p (a b c)"), in_=d.bw), w=["BW"])
    P.dma("sp", lambda e: e.dma_start(out=BC.rearrange("p a c -> p (a c)"), in_=d.bc), w=["BC"])
    P.dma("sp", lambda e: e.dma_start(out=SEL.rearrange("p a b c -> p (a b c)"), in_=d.sel), w=["SEL"])
    P.dma("sp", lambda e: e.dma_start(out=AT, in_=d.at), w=["AT"])
    P.dma("sp", lambda e: e.dma_start(out=CT, in_=d.ct), w=["CT"])
    P.dma("sp", lambda e: e.dma_start(out=b31, in_=d.b31), w=["b31"])
    P.dma("sp", lambda e: e.dma_start(out=gsig.rearrange("p a c -> p (a c)"), in_=d.tmg), w=["gsig"])
    gs2 = gsig.rearrange("p a c -> p (a c)")
    P.op("act", lambda e: e.activation(gs2, gs2, AF.Exp, scale=-1.0), r=["gsig"], w=["gsig"])
    P.op("dve", lambda e: e.tensor_scalar(gs2, gs2, 1.0, None, op0=ALU.add), r=["gsig"], w=["gsig"])
    P.op("dve", lambda e: e.reciprocal(gs2, gs2), r=["gsig"], w=["gsig"])
    for g in range(2):
        for r in range(4):
            col = b31[:, g * 4 + r:g * 4 + r + 1]
            rs = slice(r * 128, (r + 1) * 128)
            P.op("dve", lambda e, g=g, rs=rs, col=col: e.tensor_scalar(BS[:, g, :, rs], BS[:, g, :, rs], col, None,
                                                                       op0=ALU.subtract), r=["BS", "b31"], w=["BS"])
            P.op("dve", lambda e, g=g, rs=rs, col=col: e.tensor_scalar(BW[:, g, :, rs], BW[:, g, :, rs], col, None,
                                                                       op0=ALU.subtract), r=["BW", "b31"], w=["BW"])
            P.op("dve", lambda e, g=g, rs=rs, r=r: e.tensor_scalar(BC[:, g, rs], BC[:, g, rs],
                                                                   b31[0:32, g * 4 + r:g * 4 + r + 1], None,
                                                                   op0=ALU.subtract), r=["BC", "b31"], w=["BC"])
    zbank = [pb[0], pb[1]]
    otr = pb[6]
    oT_sb = al([65, 512], F32)
    ex = ex + [al([128, 512])]
    yaccs = [yacc, al([128, 512], F32)]
    m4s = [[m4[0], m4[1]], [al([128, 4, 128]), al([128, 4, 128])]]
    zr2 = [al([128, 4], F32) for _ in range(3)]
    rz2 = [al([128, 4], F32) for _ in range(3)]
    cf2 = [al([128, 4], F32) for _ in range(3)]
    units = []
    oc = [pb[2], pb[3]]
    osl = pb[4]
    owin = pb[5]
    qtok = [("QN", c) for c in range(4)]

    def mk_unit(s1, s3, add=None, after=None):
        units.append(dict(s1=s1, s3=s3, add=add, after=after))

    def add_cmp(i, g):
        qs = _sl(i, 128)
        gp = slice(g * 64, g * 64 + 64)
        qrhs = QN[gp, :, qs]
        ctmax = i // 4
        ya = yaccs[i % 2]
        for ct in range(ctmax + 1):
            slot = ct - (ctmax - 1)
            hasb = slot >= 0

            def s1(z, zt, ct=ct, slot=slot, hasb=hasb):
                P.op("pe", lambda e: e.matmul(z[:, :], lhsT=KC[gp, ct * 128:(ct + 1) * 128], rhs=qrhs, start=True,
                                              stop=(not hasb)), r=["KC"] + qtok, w=[zt])
                if hasb:
                    P.op("pe", lambda e: e.matmul(z[:, :], lhsT=SEL[:, i, slot, :], rhs=BC[:, g, :], start=False, stop=True),
                         r=["SEL", "BC"], w=[zt])

            def s3(exb, et, ct=ct):
                for r in range(4):
                    P.op("pe", lambda e, r=r: e.matmul(
                        oc[r // 2][:, (r % 2) * 193:(r % 2) * 193 + 193], lhsT=exb[:, r * 128:(r + 1) * 128],
                        rhs=RC[:, ct, g, :], start=(ct == 0 and r % 2 == 0), stop=(ct == ctmax)),
                         r=[et, ("RCo", g), ("RCv", g)], w=["oc"])

            def after_topk():
                zr, rz, coef = zr2[0], rz2[0], cf2[0]
                for b in range(2):
                    P.op("dve", lambda e, b=b: e.tensor_scalar(zr[:, 2 * b:2 * b + 2], oc[b][:, 64:64 + 386:193], 1e-30, None,
                                                               op0=ALU.add), r=["oc"], w=[("zr0", b)])
                P.op("dve", lambda e: e.reciprocal(rz, zr), r=[("zr0", 0), ("zr0", 1)], w=["rz0"])
                P.op("dve", lambda e: e.tensor_tensor(out=coef, in0=rz, in1=gsig[:, i, g * 4:g * 4 + 4], op=ALU.mult),
                     r=["rz0", "gsig"], w=["coef0"])
                for r in range(4):
                    src = oc[r // 2][:, (r % 2) * 193 + 65:(r % 2) * 193 + 193]
                    if r == 0:
                        P.op("dve", lambda e, src=src: e.tensor_scalar(score, src, rz[:, 0:1], None, op0=ALU.mult),
                             r=["oc", "rz0"], w=["score"])
                    else:
                        P.op("dve", lambda e, src=src, r=r: e.scalar_tensor_tensor(out=score, in0=src, scalar=rz[:, r:r + 1],
                                                                                   in1=score, op0=ALU.mult, op1=ALU.add),
                             r=["oc", "rz0", "score"], w=["score"])
                    hs = slice((g * 4 + r) * 64, (g * 4 + r) * 64 + 64)
                    osrc = oc[r // 2][:, (r % 2) * 193:(r % 2) * 193 + 64]
                    P.op("dve", lambda e, osrc=osrc, r=r, hs=hs: e.tensor_scalar(ya[:, hs], osrc, coef[:, r:r + 1], None,
                                                                                 op0=ALU.mult),
                         r=["oc", "coef0"], w=[("yacc", i % 2, g, r)])
                u0 = 120 - 8 * i
                P.op("dve", lambda e: e.tensor_tensor(out=score, in0=score, in1=AT[:, u0:u0 + 128], op=ALU.mult),
                     r=["score", "AT"], w=["score"])
                P.op("dve", lambda e: e.tensor_tensor(out=score, in0=score, in1=CT[:, u0:u0 + 128], op=ALU.add),
                     r=["score", "CT"], w=["score"])
                P.op("dve", lambda e: e.memset(score[:, 0:1], 1e4), r=["score"], w=["score"])
                P.op("dve", lambda e: e.max(out=mx[:, 0:8], in_=score), r=["score"], w=["mx0"])
                P.op("dve", lambda e: e.match_replace(out=scr2, in_to_replace=mx[:, 0:8], in_values=score, imm_value=-1e30),
                     r=["score", "mx0"], w=["scr2"])
                P.op("dve", lambda e: e.max(out=mx[:, 8:16], in_=scr2), r=["scr2"], w=["mx1"])
                P.op("dve", lambda e: e.tensor_scalar(m01, score, mx[:, 15:16], None, op0=ALU.is_lt),
                     r=["score", "mx1"], w=["m01"])
                P.op("pe", lambda e: e.transpose(C.ptb[:, 0:128], m01, C.ident), r=["m01", "ident"], w=["ptb"])
                P.op("dve", lambda e: e.tensor_copy(
                    m4s[i % 2][g], C.ptb[:, 0:128].rearrange("p (o q) -> p o q", o=1).to_broadcast([128, 4, 128])),
                     r=["ptb"], w=[("m4", i % 2, g)])

            mk_unit(s1, s3, None, after_topk if ct == ctmax else None)

    def fin_branch(i, g, acc, tok, br):
        ya = yaccs[i % 2]
        zr, rz, coef = zr2[br], rz2[br], cf2[br]
        P.op("dve", lambda e: e.tensor_scalar(zr, acc[:, 64:260:65], 1e-30, None, op0=ALU.add), r=[tok], w=[("zr", br)])
        P.op("dve", lambda e: e.reciprocal(rz, zr), r=[("zr", br)], w=[("rz", br)])
        gc = br * 8 + g * 4
        P.op("dve", lambda e: e.tensor_tensor(out=coef, in0=rz, in1=gsig[:, i, gc:gc + 4], op=ALU.mult),
             r=[("rz", br), "gsig"], w=[("coef", br)])
        for r in range(4):
            hs = slice((g * 4 + r) * 64, (g * 4 + r) * 64 + 64)
            P.op("dve", lambda e, r=r, hs=hs: e.scalar_tensor_tensor(
                out=ya[:, hs], in0=acc[:, r * 65:r * 65 + 64], scalar=coef[:, r:r + 1], in1=ya[:, hs],
                op0=ALU.mult, op1=ALU.add), r=[tok, ("coef", br), ("yacc", i % 2, g, r)], w=[("yacc", i % 2, g, r)])

    def fin_fm(i, g, acc, tok, br):
        P.op("act", lambda e: e.copy(oT_sb, acc[0:65, :]), r=[tok], w=["oT_sb"])
        for r in range(4):
            P.op("pe", lambda e, r=r: e.transpose(otr[:, r * 65:(r + 1) * 65], oT_sb[:, r * 128:(r + 1) * 128], C.identf[0:65, 0:65]),
                 r=["oT_sb", "identf"], w=["otr"])
        fin_branch(i, g, otr, "otr", br)

    def tile_fin(i):
        qs = _sl(i, 128)
        ya = yaccs[i % 2]
        P.op("act", lambda e: e.copy(ybf, ya), r=[("yacc", i % 2, g, r) for g in range(2) for r in range(4)], w=["ybf"])
        for c in range(4):
            P.op("pe", lambda e, c=c: e.transpose(C.ptb[:, 128 + c * 128:256 + c * 128], ybf[:, c * 128:(c + 1) * 128], C.ident),
                 r=["ybf", "ident"], w=["ptb2"])
        P.op("act", lambda e: e.copy(C.ynsT[:, :, qs], C.ptb[:, 128:640].rearrange("p (c q) -> p c q", c=4)),
             r=["ptb2"], w=[("yns", i)])

    def add_slc(i, g):
        qs = _sl(i, 128)
        gp = slice(g * 64, g * 64 + 64)
        qrhs = QN[gp, :, qs]
        m4f = m4s[i % 2][g].rearrange("p r q -> p (r q)")
        nkb = 4 * i + 4
        for kb in range(nkb):
            jj = kb - (4 * i - 1)

            def s1(z, zt, kb=kb):
                P.op("pe", lambda e: e.matmul(z[:, :], lhsT=KSL[gp, kb * 128:(kb + 1) * 128], rhs=qrhs, start=True, stop=False),
                     r=["KSL"] + qtok, w=[zt])
                P.op("pe", lambda e: e.matmul(z[:, :], lhsT=FM[:, kb * 128:(kb + 1) * 128], rhs=m4f, start=False, stop=True),
                     r=["FM", ("m4", i % 2, g)], w=[zt])

            def s3(exb, et, kb=kb):
                P.op("pe", lambda e: e.matmul(osl[0:65, :], lhsT=VSL[:, kb, g, :], rhs=exb, start=(kb == 0), stop=(kb == nkb - 1)),
                     r=[et, "VSL"], w=["osl"])

            add = (BS[:, g, jj, :], "BS") if jj >= 0 else None
            after = (lambda: fin_fm(i, g, osl, "osl", 1)) if kb == nkb - 1 else None
            mk_unit(s1, s3, add, after)

    def add_win(i, g):
        qs = _sl(i, 128)
        ws = i % 2
        gp = slice(g * 64, g * 64 + 64)
        qrhs = QN[gp, :, qs]
        for di in range(5):
            bidx = {0: 0, 3: 1, 4: 2}.get(di)

            def s1(z, zt, di=di):
                P.op("pe", lambda e: e.matmul(z[:, :], lhsT=kwn[ws][gp, di * 128:(di + 1) * 128], rhs=qrhs, start=True, stop=True),
                     r=[("kwn", ws)] + qtok, w=[zt])

            def s3(exb, et, di=di):
                P.op("pe", lambda e: e.matmul(owin[0:65, :], lhsT=vwn[ws][:, di, g, :], rhs=exb, start=(di == 0), stop=(di == 4)),
                     r=[et, ("vwn", ws)], w=["owin"])

            add = (BW[:, g, bidx, :], "BW") if bidx is not None else None
            if di == 4:
                def after(i=i, g=g):
                    fin_fm(i, g, owin, "owin", 2)
                    if g == 1:
                        tile_fin(i)
            else:
                after = None
            mk_unit(s1, s3, add, after)

    def load_win(i):
        ws = i % 2
        P.dma("sp", lambda e: e.dma_start(out=kwn[ws], in_=d.kwn[i]), w=[("kwn", ws)])
        P.dma("sp", lambda e: e.dma_start(out=vwn[ws].rearrange("p a b c -> p (a b c)"), in_=d.vwn[i]), w=[("vwn", ws)])

    marks = {}
    add_cmp(0, 0)
    add_cmp(0, 1)
    for i in range(16):
        marks[len(units)] = i
        if i + 1 < 16:
            add_cmp(i + 1, 0)
            add_cmp(i + 1, 1)
        for g in range(2):
            add_slc(i, g)
            add_win(i, g)
    nun = len(units)

    def do_s12(u):
        un = units[u]
        z = zbank[u % 2]
        zt = ("z", u % 2)
        un["s1"](z, zt)
        exb = ex[u % 3]
        et = ("ex", u % 3)
        if un["add"] is not None:
            tab, ttok = un["add"]
            tf = tmpf[u % 2]
            P.op("dve", lambda e: e.tensor_tensor(out=tf, in0=z[:, :], in1=tab, op=ALU.add), r=[zt, ttok], w=[("tmpf", u % 2)])
            P.op("act", lambda e: e.activation(exb, tf, AF.Exp), r=[("tmpf", u % 2)], w=[et])
        else:
            P.op("act", lambda e: e.activation(exb, z[:, :], AF.Exp), r=[zt], w=[et])

    def do_s3(u):
        un = units[u]
        un["s3"](ex[u % 3], ("ex", u % 3))
        if un["after"] is not None:
            un["after"]()

    for u in range(nun):
        if u in marks:
            load_win(marks[u])
        do_s12(u)
        if u >= 1:
            do_s3(u - 1)
    do_s3(nun - 1)
    AR.pop()


def phase_merge_ffn2(P, C, AR, d):
    al = AR.alloc
    pb = C.pb
    AR.push()
    x = al([128, 8, T], F32)
    h = al([128, 8, T])
    gn = al([128, 16], F32)
    C.sq = al([128, 8, 512])
    C.lnt = al([128, 512], F32)
    C.rstd = al([128, 512], F32)
    C.wgb = [al([128, 8, 128]) for _ in range(2)]
    C.wub = [al([128, 8, 128]) for _ in range(2)]
    for c in range(8):
        P.dma("sp", lambda e, c=c: e.dma_start(out=x[:, c, :], in_=d.x1T[_sl(c, 128), :]),
              w=[("x", c, tb) for tb in range(4)])
    P.dma("sp", lambda e: e.dma_start(out=gn[:, 0:8], in_=d.nm), w=["gains"])
    P.dma("sp", lambda e: e.dma_start(out=gn[:, 8:16], in_=d.n2), r=["gains"], w=["gains"])
    rmsnorm_h(P, C, x, h, gn[:, 0:8])
    AR.push()
    mg = al([128, 8, T])
    wbg = [[al([128, 8, 128]) for _ in range(3)] for _ in range(2)]
    wup = [[al([128, 4, 128]), al([128, 2, 128]), al([128, 2, 128])] for _ in range(2)]
    sgt = [al([128, 512]) for _ in range(2)]
    ysrc = [(C.ynsT, 4, "yns"), (C.ysbT, 2, "ysb"), (C.yswT, 2, "ysw")]
    ytoks = ([("yns", i) for i in range(16)] + [("ysb", cc, sbk) for cc in range(2) for sbk in range(4)]
             + [("ysw", i) for i in range(16)])
    k = 0
    for dc in range(8):
        s = dc % 2
        for br in range(3):
            P.dma("pool", lambda e, s=s, br=br, dc=dc: e.dma_start(out=wbg[s][br].rearrange("p c j -> p (c j)"),
                                                                   in_=d.wbg[dc, br]), w=[("wbg", s, br)])
            P.dma("pool", lambda e, s=s, br=br, dc=dc: e.dma_start(out=wup[s][br].rearrange("p c j -> p (c j)"),
                                                                   in_=d.wup[br][dc]), w=[("wup", s, br)])
        for tb in range(4):
            ts = _sl(tb, 512)
            for br in range(3):
                ysb_, nkc, _ = ysrc[br]
                pgt = k % 2
                pup = 2 + k % 2
                si = k % 2
                k += 1
                for c in range(8):
                    P.op("pe", lambda e, c=c, s=s, br=br, pgt=pgt, ts=ts: e.matmul(
                        pb[pgt][:, :], lhsT=wbg[s][br][:, c, :], rhs=h[:, c, ts], start=(c == 0), stop=(c == 7)),
                         r=[("wbg", s, br), ("h", c, tb)], w=[("pb", pgt)])
                for c in range(nkc):
                    P.op("pe", lambda e, c=c, s=s, br=br, pup=pup, ts=ts, ysb_=ysb_, nkc=nkc: e.matmul(
                        pb[pup][:, :], lhsT=wup[s][br][:, c, :], rhs=ysb_[:, c, ts], start=(c == 0), stop=(c == nkc - 1)),
                         r=[("wup", s, br)] + ytoks, w=[("pb", pup)])
                P.op("act", lambda e, pgt=pgt, si=si: e.activation(sgt[si], pb[pgt][:, :], AF.Sigmoid),
                     r=[("pb", pgt)], w=[("sgt", si)])
                if br == 0:
                    P.op("dve", lambda e, pup=pup, si=si, dc=dc, ts=ts: e.tensor_tensor(
                        out=mg[:, dc, ts], in0=sgt[si], in1=pb[pup][:, :], op=ALU.mult),
                         r=[("sgt", si), ("pb", pup)], w=[("mg", dc, tb)])
                else:
                    P.op("dve", lambda e, pup=pup, si=si: e.tensor_tensor(
                        out=sgt[si], in0=sgt[si], in1=pb[pup][:, :], op=ALU.mult),
                         r=[("sgt", si), ("pb", pup)], w=[("sgt", si)])
                    P.op("dve", lambda e, si=si, dc=dc, ts=ts: e.tensor_tensor(
                        out=mg[:, dc, ts], in0=mg[:, dc, ts], in1=sgt[si], op=ALU.add),
                         r=[("sgt", si), ("mg", dc, tb)], w=[("mg", dc, tb)])
    k = 0
    for dc2 in range(8):
        s = dc2 % 2
        P.dma("pool", lambda e, s=s, dc2=dc2: e.dma_start(out=C.wgb[s].rearrange("p c j -> p (c j)"), in_=d.wo[dc2]),
              w=[("wg", s)])
        for tb in range(4):
            ts = _sl(tb, 512)
            po = 4 + k % 2
            k += 1
            for c in range(8):
                P.op("pe", lambda e, c=c, s=s, po=po, ts=ts: e.matmul(pb[po][:, :], lhsT=C.wgb[s][:, c, :], rhs=mg[:, c, ts],
                                                                      start=(c == 0), stop=(c == 7)),
                     r=[("wg", s), ("mg", c, tb)], w=[("pb", po)])
            P.op("dve", lambda e, dc2=dc2, ts=ts, po=po: e.tensor_tensor(out=x[:, dc2, ts], in0=x[:, dc2, ts],
                                                                         in1=pb[po][:, :], op=ALU.add),
                 r=[("pb", po), ("x", dc2, tb)], w=[("x", dc2, tb)])
    AR.pop()
    C.wdb = [al([128, 11, 128]) for _ in range(2)]
    C.sg = [al([128, 512]) for _ in range(2)]
    act = al([128, 11, T])
    rmsnorm_h(P, C, x, h, gn[:, 8:16])
    ffn(P, C, x, h, act, d.wg, d.wu, d.wd)
    for c in range(8):
        P.dma("sp", lambda e, c=c: e.dma_start(out=d.x2T[_sl(c, 128), :], in_=x[:, c, :]),
              r=[("x", c, tb) for tb in range(4)])
    AR.pop()


def build_B(phases=("sb", "swa", "nsa", "merge")):
    nc = bass.Bass("TRN2", target_bir_lowering=False)
    dt = lambda n, s, t=F32, k="ExternalInput": nc.dram_tensor(n, s, t, kind=k).ap()
    d = NS()
    d.x1T = dt("x1T", [D, T])
    d.fmq = dt("fmq", [8, 128, T], BF16)
    d.tmg = dt("tmg", [128, 16 * 24])
    d.kt = dt("kt", [5, 128, S], BF16)
    d.vsb = dt("vsb", [2, 128, 64 * 128], BF16)
    d.vsl = dt("vsl", [128, 64 * 130], BF16)
    d.kwn = dt("kwn", [16, 128, 640], BF16)
    d.vwn = dt("vwn", [16, 128, 650], BF16)
    d.kw2 = dt("kw2", [128, 16, 256], BF16)
    d.vw2 = dt("vw2", [128, 16 * 260], BF16)
    d.msb = dt("msb", [128, 4, 128], BF16)
    d.bs = dt("bs", [128, 2 * 5 * 512])
    d.bw = dt("bw", [128, 2 * 3 * 512])
    d.bsw = dt("bsw", [128, 2 * 2 * 256])
    d.bc = dt("bc", [32, 2 * 512])
    d.sel = dt("sel", [32, 16 * 2 * 128])
    d.at = dt("at", [128, 248])
    d.ct = dt("ct", [128, 248])
    d.fmask = dt("fmask", [128, S], BF16)
    d.ovl = dt("ovl", [128, 4, 129], BF16)
    d.b31 = dt("b31", [128, 8])
    d.sinks = dt("sinks", [128, 4])
    d.w1 = [dt("w1k", [128, 32 * 256]), dt("w1v", [128, 32 * 256])]
    d.w2 = [dt("w2k", [128, 128]), dt("w2v", [128, 128])]
    d.post = dt("post", [128, 32])
    d.gk = dt("gk", [128, 1])
    d.nm = dt("nm", [128, 8])
    d.n2 = dt("n2", [128, 8])
    d.wbg = dt("wbg", [8, 3, 128, 1024])
    d.wup = [dt("wupn", [8, 128, 512]), dt("wups", [8, 128, 256]), dt("wupw", [8, 128, 256])]
    d.wo = dt("wo", [8, 128, 1024])
    d.wg = dt("wg", [22, 128, 1024])
    d.wu = dt("wu", [22, 128, 1024])
    d.wd = dt("wd", [2, 8, 128, 11 * 128])
    d.x2T = dt("x2T", [D, T], F32, "ExternalOutput")
    d.ydbg = dt("ydbg", [8, 128, T], BF16, "ExternalOutput")

    P = Prog(nc)
    C = NS()
    AR = Arena(P, 211000)
    al = AR.alloc
    C.pb = [P.ps("pb%d" % i) for i in range(7)]
    C.ptb = P.ps("ptb", [128, 1024], BF16)
    C.ones = al([128, 128])
    C.bd = al([128, 128])
    C.ident = al([128, 128])
    C.nu = al([128, 128])
    C.negone = al([1, 128])
    C.ynsT = al([128, 4, T])
    C.ysbT = al([128, 2, T])
    C.yswT = al([128, 2, T])
    P.op("dve", lambda e: e.memset(C.ones, 1.0), w=["ones"])
    P.op("dve", lambda e: e.memset(C.bd, 0.0), w=["bd"])
    P.op("dve", lambda e: e.memset(C.bd[0:64, 0:64], 1.0), r=["bd"], w=["bd"])
    P.op("dve", lambda e: e.memset(C.bd[64:128, 64:128], 1.0), r=["bd"], w=["bd"])
    P.op("dve", lambda e: e.memset(C.ident, 1.0), w=["ident"])
    P.op("pool", lambda e: e.affine_select(out=C.ident, in_=C.ident, pattern=[[-1, 128]], compare_op=ALU.is_equal,
                                           fill=0.0, base=0, channel_multiplier=1), r=["ident"], w=["ident"])
    C.identf = al([128, 128], F32)
    P.op("dve", lambda e: e.memset(C.identf, 1.0), w=["identf"])
    P.op("pool", lambda e: e.affine_select(out=C.identf, in_=C.identf, pattern=[[-1, 128]], compare_op=ALU.is_equal,
                                           fill=0.0, base=0, channel_multiplier=1), r=["identf"], w=["identf"])
    P.op("dve", lambda e: e.memset(C.nu, -1.0), w=["nu"])
    P.op("pool", lambda e: e.affine_select(out=C.nu, in_=C.nu, pattern=[[-1, 128]], compare_op=ALU.is_ge,
                                           fill=0.0, base=0, channel_multiplier=1), r=["nu"], w=["nu"])
    P.op("dve", lambda e: e.memset(C.negone, -1.0), w=["negone"])
    for nm_, t_ in (("ynsT", C.ynsT), ("ysbT", C.ysbT), ("yswT", C.yswT)):
        if {"ynsT": "nsa", "ysbT": "sb", "yswT": "swa"}[nm_] not in phases:
            P.op("dve", lambda e, t_=t_: e.memset(t_.rearrange("p c t -> p (c t)"), 0.0),
                 w=[(nm_[:3], i) for i in range(16)] + [(nm_[:3], a, b) for a in range(2) for b in range(4)])
    if "sb" in phases:
        phase_sb(P, C, AR, d)
    if "swa" in phases:
        phase_swa(P, C, AR, d)
    if "nsa" in phases:
        phase_nsa(P, C, AR, d)
    ytoks = ([("yns", i) for i in range(16)] + [("ysb", cc, sbk) for cc in range(2) for sbk in range(4)]
             + [("ysw", i) for i in range(16)])
    for c in range(4):
        P.dma("sp", lambda e, c=c: e.dma_start(out=d.ydbg[c], in_=C.ynsT[:, c, :]), r=ytoks)
    for c in range(2):
        P.dma("sp", lambda e, c=c: e.dma_start(out=d.ydbg[4 + c], in_=C.ysbT[:, c, :]), r=ytoks)
        P.dma("sp", lambda e, c=c: e.dma_start(out=d.ydbg[6 + c], in_=C.yswT[:, c, :]), r=ytoks)
    if "merge" in phases:
        phase_merge_ffn2(P, C, AR, d)
    P.finish()
    P.emit()
    return nc


BF = ml_dtypes.bfloat16
QCH = [0, 1, 2, 3, 8, 9, 12, 13]
KCH = [4, 5, 6, 10, 11]


def t5_bucket_np(dd):
    dd = np.maximum(dd, 0)
    df = np.maximum(dd, 1).astype(np.float32)
    large = 16 + (np.log(df / np.float32(16)) / np.float32(math.log(8.0)) * np.float32(16)).astype(np.int32)
    large = np.minimum(large, 31)
    return np.where(dd < 16, dd, large)


def bias_tab(rel_bias, dist, col, valid):
    return np.where(valid, rel_bias[t5_bucket_np(dist), col], np.float32(NEG)).astype(np.float32)


def core_tables(rel_bias, rc):
    tq = np.arange(128)[None, :]
    sk = np.arange(128)[:, None]
    t = {}
    bs = np.zeros((128, 2, 5, 4, 128), np.float32)
    bw = np.zeros((128, 2, 3, 4, 128), np.float32)
    bsw = np.zeros((128, 2, 2, 2, 128), np.float32)
    bc = np.full((32, 2, 4, 128), NEG, np.float32)
    for g in range(2):
        for r in range(4):
            col = g * 4 + r
            for jj in range(5):
                dist = (rc - (jj - 1)) * 128 + tq - sk
                bs[:, g, jj, r, :] = bias_tab(rel_bias, dist, col, dist >= 0)
            for bi, dl in enumerate((4, 1, 0)):
                dist = dl * 128 + tq - sk
                bw[:, g, bi, r, :] = bias_tab(rel_bias, dist, col, (dist >= 0) & (dist < 512))
            for m in range(31):
                dd = tq[0] - 16 * (m - 24) - 31
                bc[m, g, r, :] = bias_tab(rel_bias, dd, col, dd >= 0)
        for r in range(2):
            col = 8 + g * 2 + r
            for di in range(2):
                dist = (1 - di) * 128 + tq - sk
                bsw[:, di, g, r, :] = bias_tab(rel_bias, dist, col, (dist >= 0) & (dist < 128))
    t["bs"] = bs.reshape(128, -1)
    t["bw"] = bw.reshape(128, -1)
    t["bsw"] = bsw.reshape(128, -1)
    t["bc"] = bc.reshape(32, -1)
    sel = np.zeros((32, 16, 2, 128), np.float32)
    nk = np.arange(128)
    for i in range(16):
        gt = 4 * i + rc
        ctmax = i // 4
        for slot in range(2):
            ct = ctmax - 1 + slot
            if ct < 0:
                continue
            n = 128 * ct + nk
            rel = n - 8 * gt
            m = np.where((rel >= 7) | (n >= 511), 31, np.where(rel >= -24, rel + 24, -1))
            ok = m >= 0
            sel[m[ok], i, slot, nk[ok]] = 1.0
    t["sel"] = sel.reshape(32, -1)
    u = np.arange(248)[None, :]
    c0 = (np.arange(128)[:, None] >= 64).astype(np.int64)
    rel = u - 120 - 2 * rc
    t["at"] = (rel <= c0 - 2).astype(np.float32)
    t["ct"] = np.where((rel == c0) | (rel == c0 - 1), np.float32(1e4),
                       np.where(rel > c0, np.float32(-1.0), np.float32(0.0))).astype(np.float32)
    msb = np.zeros((128, 4, 128), np.float32)
    for j in range(4):
        if j < rc:
            msb[:, j, :] = 1.0
        elif j == rc:
            msb[:, j, :] = (sk < tq)
    t["msb"] = msb.astype(BF)
    return t


def const_tables():
    fmask = np.where((np.arange(S)[None, :] // 64) == np.arange(128)[:, None], np.float32(NEG), np.float32(0)).astype(BF)
    ovl = np.zeros((128, 4, 129), np.float32)
    for ct in range(4):
        n = 128 * ct + np.arange(128)
        j = np.arange(128)
        ok = (n[:, None] < 511) & (16 * n[:, None] < 64 * j[None, :] + 64) & (16 * n[:, None] + 31 >= 64 * j[None, :])
        ovl[:, ct, 1:] = ok
        ovl[:, ct, 0] = (n < 511)
    return dict(fmask=fmask, ovl=ovl.astype(BF))


def gather_global(resA, b):
    fm = np.stack([np.asarray(resA[b * 4 + r]["fm"]).reshape(15, 128, 16, 128) for r in range(4)], axis=3).reshape(15, 128, S)
    tm = np.stack([np.asarray(resA[b * 4 + r]["tm"]).reshape(16, 128, 640) for r in range(4)], axis=1).reshape(S, 640)
    return fm, tm


def tile_or_zero(arr, tile, axis):
    if tile < 0:
        shp = list(arr.shape)
        shp[axis] = 128
        return np.zeros(shp, arr.dtype)
    sl = [slice(None)] * arr.ndim
    sl[axis] = slice(tile * 128, (tile + 1) * 128)
    return arr[tuple(sl)]


def vaug(v):
    o = np.ones((128, 2, 65), v.dtype)
    o[:, :, :64] = v.reshape(128, 2, 64)
    return o


def prep_B_batch(fm, tm):
    sh = {}
    sh["kt"] = np.ascontiguousarray(fm[KCH])
    sv = tm[:, 256:512]
    sh["vsb"] = np.ascontiguousarray(sv.reshape(64, 128, 2, 128).transpose(2, 1, 0, 3)).reshape(2, 128, 64 * 128)
    nvs = tm[:, 0:128].reshape(64, 128, 2, 64)
    vsl = np.ones((128, 64, 2, 65), tm.dtype)
    vsl[:, :, :, :64] = nvs.transpose(1, 0, 2, 3)
    sh["vsl"] = vsl.reshape(128, -1)
    return sh


def prep_B_core(fm, tm, rc, resA_c, x1T):
    o = {}
    nkw, wk = fm[7], fm[14]
    nvw, wv = tm[:, 128:256], tm[:, 512:640]
    kwn = np.zeros((16, 128, 640), fm.dtype)
    vwn = np.zeros((16, 128, 5, 2, 65), tm.dtype)
    kw2 = np.zeros((128, 16, 256), fm.dtype)
    vw2 = np.zeros((128, 16, 2, 2, 65), tm.dtype)
    for i in range(16):
        gt = 4 * i + rc
        for di in range(5):
            tl = gt - 4 + di
            if tl >= 0:
                kwn[i, :, di * 128:(di + 1) * 128] = nkw[:, tl * 128:(tl + 1) * 128]
                vwn[i, :, di] = vaug(nvw[tl * 128:(tl + 1) * 128])
        for di in range(2):
            tl = gt - 1 + di
            if tl >= 0:
                kw2[:, i, di * 128:(di + 1) * 128] = wk[:, tl * 128:(tl + 1) * 128]
                vw2[:, i, di] = vaug(wv[tl * 128:(tl + 1) * 128])
    o["kwn"] = kwn
    o["vwn"] = vwn.reshape(16, 128, 650)
    o["kw2"] = kw2
    o["vw2"] = vw2.reshape(128, -1)
    o["fmq"] = np.ascontiguousarray(np.asarray(resA_c["fm"])[QCH])
    o["tmg"] = np.ascontiguousarray(np.asarray(resA_c["tmg"]).reshape(16, 128, 24).transpose(1, 0, 2)).reshape(128, -1)
    o["x1T"] = x1T
    return o


def prep_B_weights(inp, l):
    w = {}
    rep = lambda a: np.ascontiguousarray(np.concatenate([a, a], axis=0))
    for nm_, key in (("w1k", "nsa_cmp_k_w1"), ("w1v", "nsa_cmp_v_w1")):
        w1 = inp[key][l].reshape(32, 64, 256).transpose(1, 0, 2).reshape(64, 32 * 256)
        w[nm_] = rep(w1)
    for nm_, key in (("w2k", "nsa_cmp_k_w2"), ("w2v", "nsa_cmp_v_w2")):
        w[nm_] = np.ascontiguousarray(inp[key][l].reshape(2, 128, 64).transpose(1, 0, 2)).reshape(128, 128)
    w["post"] = rep(np.ascontiguousarray(inp["nsa_cmp_pos"][l].T))
    w["gk"] = rep(inp["nsa_k_norm"][l].reshape(64, 1))
    w["nm"] = vec8(inp["mix_norm"][l])
    w["n2"] = vec8(inp["ffn2_norm"][l])
    w_in = inp["w_in"][l]
    bg = w_in[:, O_BG:].reshape(8, 128, 3, 8, 128)
    w["wbg"] = np.ascontiguousarray(bg.transpose(3, 2, 1, 0, 4)).reshape(8, 3, 128, 1024)
    for nm_, key, nkc in (("wupn", "w_up_nsa", 4), ("wups", "w_up_sb", 2), ("wupw", "w_up_swa", 2)):
        u = inp[key][l].reshape(nkc, 128, 8, 128)
        w[nm_] = np.ascontiguousarray(u.transpose(2, 1, 0, 3)).reshape(8, 128, nkc * 128)
    wo = inp["w_out"][l].reshape(8, 128, 8, 128)
    w["wo"] = np.ascontiguousarray(wo.transpose(2, 1, 0, 3)).reshape(8, 128, 1024)
    w["wg"], w["wu"], w["wd"] = prep_ffn(inp["ffn2_w_gate"][l], inp["ffn2_w_up"][l], inp["ffn2_w_down"][l])
    w["b31"] = np.ascontiguousarray(np.broadcast_to(inp["rel_bias"][31, 0:8], (128, 8)))
    w["sinks"] = np.ascontiguousarray(np.broadcast_to(inp["swa_sinks"][l], (128, 4)))
    return w


_CACHE = {}


def get_nc(name):
    if name not in _CACHE:
        _CACHE[name] = build_A() if name == "A" else build_B()
    return _CACHE[name]


def run_layer(inp, l, xTs, tabs, consts):
    ncA = get_nc("A")
    pa = prep_A(inp, l)
    resA = run_bass_kernel_spmd(ncA, [dict(pa, xT=xTs[c]) for c in range(8)], core_ids=list(range(8))).results
    wB = prep_B_weights(inp, l)
    maps = []
    for b in range(2):
        fm, tm = gather_global(resA, b)
        sh = prep_B_batch(fm, tm)
        for rc in range(4):
            c = b * 4 + rc
            m = dict(wB)
            m.update(consts)
            m.update(tabs[rc])
            m.update(sh)
            m.update(prep_B_core(fm, tm, rc, resA[c], np.asarray(resA[c]["x1T"])))
            maps.append(m)
    ncB = get_nc("B")
    resB = run_bass_kernel_spmd(ncB, maps, core_ids=list(range(8))).results
    return [np.asarray(resB[c]["x2T"]) for c in range(8)], resA, resB


def kernel(**inp):
    inp = {k: np.asarray(v) for k, v in inp.items()}
    xTs = shard_xT(inp["x"].astype(np.float32))
    tabs = [core_tables(inp["rel_bias"], rc) for rc in range(4)]
    consts = const_tables()
    for l in range(2):
        xTs, _, _ = run_layer(inp, l, xTs, tabs, consts)
    return unshard_xT(xTs)
```

```python
import contextlib
import math
import numpy as np
import ml_dtypes
import concourse.bass as bass
import concourse.mybir as mybir
from concourse.bass_utils import run_bass_kernel_spmd

F32 = mybir.dt.float32
BF16 = mybir.dt.bfloat16
AF = mybir.ActivationFunctionType
ALU = mybir.AluOpType
AX = mybir.AxisListType

NDMA_SLOTS = 8
EPS = 1e-6
NEG = -30000.0
D = 1024
DFF = 2816
T = 2048
S = 8192
NT = 16
LN8 = math.log(0.125)

O_NQ, O_NKC, O_NVC, O_NKS, O_NVS, O_NKW, O_NVW, O_NG = 0, 512, 640, 768, 896, 1024, 1152, 1280
O_SQ, O_SK, O_SV, O_WQ, O_WK, O_WV, O_BG = 1304, 1560, 1816, 2072, 2328, 2456, 2584


class _Rec:
    def __getattr__(self, name):
        return lambda *a, **k: (name, a, k)


_REC = _Rec()


class Prog:
    ENGS = ["pe", "act", "dve", "pool", "sp"]

    def __init__(self, nc):
        self.nc = nc
        self.ops = {e: [] for e in self.ENGS}
        self.cnt = {e: 0 for e in self.ENGS}
        self.dcnt = {e: 0 for e in self.ENGS}
        self.lastw = {}
        self.readers = {}
        self.waited = {e: {} for e in self.ENGS}
        self.stack = contextlib.ExitStack()
        self.semh = {}

    def sb(self, name, shape, dt):
        return self.stack.enter_context(self.nc.sbuf_tensor("sb_" + name, list(shape), dt))

    def ps(self, name, shape=(128, 512), dt=F32):
        return self.stack.enter_context(self.nc.psum_tensor("ps_" + name, list(shape), dt))

    def sem(self, key):
        if key not in self.semh:
            nm = "s_" + "_".join(str(k) for k in key)
            self.semh[key] = self.stack.enter_context(self.nc.semaphore(nm))
        return self.semh[key]

    def _need(self, eng, r, w):
        need = {}

        def add(sk, v):
            if v > need.get(sk, 0):
                need[sk] = v

        for t in r:
            if t in self.lastw:
                add(*self.lastw[t])
        for t in w:
            if t in self.lastw:
                add(*self.lastw[t])
            for sk, v in self.readers.get(t, ()):
                add(sk, v)
        out = []
        for sk, v in need.items():
            if sk == ("c", "pe") and eng == "pe":
                continue
            if self.waited[eng].get(sk, 0) >= v:
                continue
            self.waited[eng][sk] = v
            out.append((sk, v))
        return out

    def _commit(self, ident, r, w):
        for t in r:
            self.readers.setdefault(t, []).append(ident)
        for t in w:
            self.lastw[t] = ident
            self.readers[t] = []

    def op(self, eng, fn, r=(), w=()):
        waits = self._need(eng, r, w)
        self.cnt[eng] += 1
        ident = (("c", eng), self.cnt[eng])
        self.ops[eng].append((waits, fn(_REC), (("c", eng), 1)))
        self._commit(ident, r, w)

    def dma(self, eng, fn, r=(), w=()):
        k = self.dcnt[eng]
        self.dcnt[eng] += 1
        slot = k % NDMA_SLOTS
        sk = ("d", eng, slot)
        waits = self._need(eng, r, w)
        prev = 16 * (k // NDMA_SLOTS)
        if prev > 0 and self.waited[eng].get(sk, 0) < prev:
            self.waited[eng][sk] = prev
            waits.append((sk, prev))
        ident = (sk, prev + 16)
        self.ops[eng].append((waits, fn(_REC), (sk, 16)))
        self._commit(ident, r, w)

    def finish(self):
        for eng in self.ENGS:
            k = self.dcnt[eng]
            for slot in range(min(k, NDMA_SLOTS)):
                n = (k - 1 - slot) // NDMA_SLOTS + 1
                sk = ("d", eng, slot)
                if self.waited["sp"].get(sk, 0) < 16 * n:
                    self.waited["sp"][sk] = 16 * n
                    self.ops["sp"].append(([(sk, 16 * n)], None, None))
        for eng in ("pe", "act", "dve", "pool"):
            if self.cnt[eng] > 0:
                self.ops["sp"].append(([(("c", eng), self.cnt[eng])], None, None))

    def emit(self):
        nc = self.nc
        for eng in self.ENGS:
            for waits, fn, inc in self.ops[eng]:
                for sk, v in waits:
                    self.sem(sk)
                if inc is not None:
                    self.sem(inc[0])
        P = self

        def run(engname):
            def body(e):
                for waits, fn, inc in P.ops[engname]:
                    for sk, v in waits:
                        e.wait_ge(P.semh[sk], v)
                    if fn is not None:
                        ins = getattr(e, fn[0])(*fn[1], **fn[2])
                        ins.then_inc(P.semh[inc[0]], inc[1])
            return body

        with nc.Block() as block:
            block.tensor(run("pe"))
            block.scalar(run("act"))
            block.vector(run("dve"))
            block.gpsimd(run("pool"))
            block.sync(run("sp"))
        self.stack.close()


class NS:
    pass


def _sl(i, n):
    return slice(i * n, (i + 1) * n)


def setup_common(P, C):
    C.ones = P.sb("ones", [128, 128], BF16)
    C.bd = P.sb("bd", [128, 128], BF16)
    P.op("dve", lambda e: e.memset(C.ones[:], 1.0), w=["ones"])
    P.op("dve", lambda e: e.memset(C.bd[:], 0.0), w=["bd"])
    P.op("dve", lambda e: e.memset(C.bd[0:64, 0:64], 1.0), r=["bd"], w=["bd"])
    P.op("dve", lambda e: e.memset(C.bd[64:128, 64:128], 1.0), r=["bd"], w=["bd"])
    C.pb = [P.ps("pb%d" % i) for i in range(8)]
    C.sq = P.sb("sq", [128, 8, 512], BF16)
    C.lnt = P.sb("lnt", [128, 512], F32)
    C.rstd = P.sb("rstd", [128, 512], F32)
    C.wgb = [P.sb("wgb%d" % i, [128, 8, 128], BF16) for i in range(2)]
    C.wub = [P.sb("wub%d" % i, [128, 8, 128], BF16) for i in range(2)]
    C.wdb = [P.sb("wdb%d" % i, [128, 11, 128], BF16) for i in range(2)]
    C.sg = [P.sb("sg%d" % i, [128, 512], BF16) for i in range(2)]


def rmsnorm_h(P, C, x, h, gain, tbs=range(4)):
    for tb in tbs:
        ts = _sl(tb, 512)
        for c in range(8):
            P.op("act", lambda e, c=c, ts=ts: e.activation(C.sq[:, c, :], x[:, c, ts], AF.Square),
                 r=[("x", c, tb)], w=[("sq", c)])
        for c in range(8):
            P.op("pe", lambda e, c=c: e.matmul(C.pb[6][:], lhsT=C.ones[:], rhs=C.sq[:, c, :],
                                               start=(c == 0), stop=(c == 7)),
                 r=[("sq", c), "ones"], w=[("pb", 6)])
        P.op("act", lambda e: e.activation(C.lnt[:], C.pb[6][:], AF.Ln, bias=EPS, scale=1.0 / D),
             r=[("pb", 6)], w=["lnt"])
        P.op("act", lambda e: e.activation(C.rstd[:], C.lnt[:], AF.Exp, scale=-0.5), r=["lnt"], w=["rstd"])
        for c in range(8):
            P.op("dve", lambda e, c=c, ts=ts: e.scalar_tensor_tensor(
                out=h[:, c, ts], in0=x[:, c, ts], scalar=gain[:, c:c + 1], in1=C.rstd[:],
                op0=ALU.mult, op1=ALU.mult),
                 r=[("x", c, tb), "rstd", "gains"], w=[("h", c, tb)])


def ffn(P, C, x, h, act, wg_d, wu_d, wd_d):
    k = 0
    k2 = 0
    for half in range(2):
        for fl in range(11):
            f = half * 11 + fl
            s = f % 2
            P.dma("pool", lambda e, s=s, f=f: e.dma_start(out=C.wgb[s][:].rearrange("p c j -> p (c j)"), in_=wg_d[f]),
                  w=[("wg", s)])
            P.dma("pool", lambda e, s=s, f=f: e.dma_start(out=C.wub[s][:].rearrange("p c j -> p (c j)"), in_=wu_d[f]),
                  w=[("wu", s)])
            for tb in range(4):
                ts = _sl(tb, 512)
                pg = k % 2
                pu = 2 + k % 2
                sgi = k % 2
                k += 1
                for c in range(8):
                    P.op("pe", lambda e, c=c, s=s, ts=ts, pg=pg: e.matmul(
                        C.pb[pg][:], lhsT=C.wgb[s][:, c, :], rhs=h[:, c, ts], start=(c == 0), stop=(c == 7)),
                         r=[("wg", s), ("h", c, tb)], w=[("pb", pg)])
                for c in range(8):
                    P.op("pe", lambda e, c=c, s=s, ts=ts, pu=pu: e.matmul(
                        C.pb[pu][:], lhsT=C.wub[s][:, c, :], rhs=h[:, c, ts], start=(c == 0), stop=(c == 7)),
                         r=[("wu", s), ("h", c, tb)], w=[("pb", pu)])
                P.op("act", lambda e, pg=pg, sgi=sgi: e.activation(C.sg[sgi][:], C.pb[pg][:], AF.Silu),
                     r=[("pb", pg)], w=[("sg", sgi)])
                P.op("dve", lambda e, pu=pu, sgi=sgi, fl=fl, ts=ts: e.tensor_tensor(
                    out=act[:, fl, ts], in0=C.sg[sgi][:], in1=C.pb[pu][:], op=ALU.mult),
                     r=[("sg", sgi), ("pb", pu)], w=[("act", fl, tb)])
        for dc in range(8):
            s = dc % 2
            P.dma("pool", lambda e, s=s, half=half, dc=dc: e.dma_start(
                out=C.wdb[s][:].rearrange("p f j -> p (f j)"), in_=wd_d[half, dc]), w=[("wd", s)])
            for tb in range(4):
                ts = _sl(tb, 512)
                po = 4 + k2 % 2
                k2 += 1
                for fl in range(11):
                    P.op("pe", lambda e, fl=fl, s=s, ts=ts, po=po: e.matmul(
                        C.pb[po][:], lhsT=C.wdb[s][:, fl, :], rhs=act[:, fl, ts], start=(fl == 0), stop=(fl == 10)),
                         r=[("wd", s), ("act", fl, tb)], w=[("pb", po)])
                P.op("dve", lambda e, dc=dc, ts=ts, po=po: e.scalar_tensor_tensor(
                    out=x[:, dc, ts], in0=C.pb[po][:], scalar=0.5, in1=x[:, dc, ts], op0=ALU.mult, op1=ALU.add),
                     r=[("pb", po), ("x", dc, tb)], w=[("x", dc, tb)])


FM_KINDS = ([("n", 0, True)] * 4 + [("p", 1.0), ("p", 1.0), ("n", 1, False), ("n", 1, False)]
            + [("p", 0.125)] * 2 + [("p", 1.0)] * 2 + [("n", 2, True)] * 2 + [("n", 3, False)])


def build_A():
    nc = bass.Bass("TRN2", target_bir_lowering=False)
    dt = lambda n, s, t=F32, k="ExternalInput": nc.dram_tensor(n, s, t, kind=k).ap()
    xT_d = dt("xT", [D, T])
    n1_d = dt("n1", [128, 8])
    nm_d = dt("nm", [128, 8])
    wg_d = dt("wg", [22, 128, 1024])
    wu_d = dt("wu", [22, 128, 1024])
    wd_d = dt("wd", [2, 8, 128, 11 * 128])
    wfm_d = dt("wfm", [15, 128, 1024])
    wtm_d = dt("wtm", [8, 128, 664])
    gfm_d = dt("gfm", [128, 4])
    x1T_d = dt("x1T", [D, T], F32, "ExternalOutput")
    fm_d = dt("fm", [15, 128, T], BF16, "ExternalOutput")
    tm_d = dt("tm", [T, 640], BF16, "ExternalOutput")
    tmg_d = dt("tmg", [T, 24], F32, "ExternalOutput")

    P = Prog(nc)
    C = NS()
    setup_common(P, C)
    x = P.sb("x", [128, 8, T], F32)
    h = P.sb("h", [128, 8, T], BF16)
    act = P.sb("act", [128, 11, T], BF16)
    gn = P.sb("gn", [128, 20], F32)
    for c in range(8):
        P.dma("sp", lambda e, c=c: e.dma_start(out=x[:, c, :], in_=xT_d[_sl(c, 128), :]),
              w=[("x", c, tb) for tb in range(4)])
    P.dma("sp", lambda e: e.dma_start(out=gn[:, 0:8], in_=n1_d), w=["gains"])
    P.dma("sp", lambda e: e.dma_start(out=gn[:, 8:16], in_=nm_d), r=["gains"], w=["gains"])
    P.dma("sp", lambda e: e.dma_start(out=gn[:, 16:20], in_=gfm_d), r=["gains"], w=["gains"])

    rmsnorm_h(P, C, x, h, gn[:, 0:8])
    ffn(P, C, x, h, act, wg_d, wu_d, wd_d)
    rmsnorm_h(P, C, x, h, gn[:, 8:16])

    wtm = P.sb("wtm", [128, 8, 664], BF16)
    for c in range(8):
        P.dma("pool", lambda e, c=c: e.dma_start(out=wtm[:, c, :], in_=wtm_d[c]), w=[("wtm", c)])
    ofm = [P.sb("ofm%d" % i, [128, 512], BF16) for i in range(2)]
    sqb = P.sb("sqb", [128, 512], BF16)
    k = 0
    for ch in range(15):
        s = ch % 2
        kind = FM_KINDS[ch]
        P.dma("pool", lambda e, s=s, ch=ch: e.dma_start(out=C.wgb[s][:].rearrange("p c j -> p (c j)"), in_=wfm_d[ch]),
              w=[("wg", s)])
        for tb in range(4):
            ts = _sl(tb, 512)
            pz = k % 2
            pn = 2 + k % 2
            oi = k % 2
            k += 1
            for c in range(8):
                P.op("pe", lambda e, c=c, s=s, ts=ts, pz=pz: e.matmul(
                    C.pb[pz][:], lhsT=C.wgb[s][:, c, :], rhs=h[:, c, ts], start=(c == 0), stop=(c == 7)),
                     r=[("wg", s), ("h", c, tb)], w=[("pb", pz)])
            if kind[0] == "n":
                P.op("act", lambda e, pz=pz: e.activation(sqb[:], C.pb[pz][:], AF.Square), r=[("pb", pz)], w=["sqb"])
                P.op("pe", lambda e, pn=pn: e.matmul(C.pb[pn][:], lhsT=C.bd[:], rhs=sqb[:], start=True, stop=True),
                     r=["sqb", "bd"], w=[("pb", pn)])
                P.op("act", lambda e, pn=pn: e.activation(C.lnt[:], C.pb[pn][:], AF.Ln, bias=EPS, scale=1.0 / 64),
                     r=[("pb", pn)], w=["lnt"])
                bias = LN8 if kind[2] else 0.0
                P.op("act", lambda e, bias=bias: e.activation(C.rstd[:], C.lnt[:], AF.Exp, scale=-0.5, bias=bias),
                     r=["lnt"], w=["rstd"])
                gc = 16 + kind[1]
                P.op("dve", lambda e, pz=pz, oi=oi, gc=gc: e.scalar_tensor_tensor(
                    out=ofm[oi][:], in0=C.pb[pz][:], scalar=gn[:, gc:gc + 1], in1=C.rstd[:],
                    op0=ALU.mult, op1=ALU.mult), r=[("pb", pz), "rstd", "gains"], w=[("ofm", oi)])
            else:
                P.op("act", lambda e, pz=pz, oi=oi, sc=kind[1]: e.mul(ofm[oi][:], C.pb[pz][:], sc),
                     r=[("pb", pz)], w=[("ofm", oi)])
            P.dma("sp", lambda e, ch=ch, ts=ts, oi=oi: e.dma_start(out=fm_d[ch, :, ts], in_=ofm[oi][:]),
                  r=[("ofm", oi)])
    otm = [P.sb("otm%d" % i, [128, 640], BF16) for i in range(2)]
    otg = [P.sb("otg%d" % i, [128, 24], F32) for i in range(2)]
    for tt in range(16):
        tsl = _sl(tt, 128)
        tb = tt // 4
        s = tt % 2
        pa = 4 + s
        pbn = 6 + s
        for c in range(8):
            P.op("pe", lambda e, c=c, tsl=tsl, pa=pa: e.matmul(
                C.pb[pa][:, 0:512], lhsT=h[:, c, tsl], rhs=wtm[:, c, 0:512], start=(c == 0), stop=(c == 7)),
                 r=[("wtm", c), ("h", c, tb)], w=[("pb", pa)])
        for c in range(8):
            P.op("pe", lambda e, c=c, tsl=tsl, pbn=pbn: e.matmul(
                C.pb[pbn][:, 0:152], lhsT=h[:, c, tsl], rhs=wtm[:, c, 512:664], start=(c == 0), stop=(c == 7)),
                 r=[("wtm", c), ("h", c, tb)], w=[("pb", pbn)])
        P.op("act", lambda e, s=s, pa=pa: e.copy(otm[s][:, 0:512], C.pb[pa][:, 0:512]),
             r=[("pb", pa)], w=[("otm", s, 0)])
        P.op("dve", lambda e, s=s, pbn=pbn: e.tensor_copy(otm[s][:, 512:640], C.pb[pbn][:, 0:128]),
             r=[("pb", pbn)], w=[("otm", s, 1)])
        P.op("dve", lambda e, s=s, pbn=pbn: e.tensor_copy(otg[s][:], C.pb[pbn][:, 128:152]),
             r=[("pb", pbn)], w=[("otg", s)])
        P.dma("sp", lambda e, s=s, tsl=tsl: e.dma_start(out=tm_d[tsl, :], in_=otm[s][:]),
              r=[("otm", s, 0), ("otm", s, 1)])
        P.dma("sp", lambda e, s=s, tsl=tsl: e.dma_start(out=tmg_d[tsl, :], in_=otg[s][:]), r=[("otg", s)])
    for c in range(8):
        P.dma("sp", lambda e, c=c: e.dma_start(out=x1T_d[_sl(c, 128), :], in_=x[:, c, :]),
              r=[("x", c, tb) for tb in range(4)])
    P.finish()
    P.emit()
    return nc


def fm_cols():
    cols = []
    for r in range(4):
        cols += [O_NQ + (0 * 4 + r) * 64 + d for d in range(64)] + [O_NQ + (1 * 4 + r) * 64 + d for d in range(64)]
    cols += list(range(O_NKC, O_NKC + 128)) + list(range(O_NVC, O_NVC + 128))
    cols += list(range(O_NKS, O_NKS + 128)) + list(range(O_NKW, O_NKW + 128))
    cols += list(range(O_SQ, O_SQ + 256)) + list(range(O_SK, O_SK + 256))
    for r in range(2):
        cols += [O_WQ + (0 * 2 + r) * 64 + d for d in range(64)] + [O_WQ + (1 * 2 + r) * 64 + d for d in range(64)]
    cols += list(range(O_WK, O_WK + 128))
    return np.array(cols)


def tm_cols():
    return np.array(list(range(O_NVS, O_NVS + 128)) + list(range(O_NVW, O_NVW + 128))
                    + list(range(O_SV, O_SV + 256)) + list(range(O_WV, O_WV + 128)) + list(range(O_NG, O_NG + 24)))


def kmajor(w, nk):
    return np.ascontiguousarray(w.reshape(nk, 128, -1).transpose(1, 0, 2))


def prep_ffn(wg, wu, wd):
    wg_a = np.ascontiguousarray(wg.reshape(8, 128, 22, 128).transpose(2, 1, 0, 3)).reshape(22, 128, 1024)
    wu_a = np.ascontiguousarray(wu.reshape(8, 128, 22, 128).transpose(2, 1, 0, 3)).reshape(22, 128, 1024)
    wd_a = np.ascontiguousarray(wd.reshape(2, 11, 128, 8, 128).transpose(0, 3, 2, 1, 4)).reshape(2, 8, 128, 11 * 128)
    return wg_a, wu_a, wd_a


def vec8(v):
    return np.ascontiguousarray(v.reshape(8, 128).T)


def prep_A(inp, l):
    w_in = inp["w_in"][l]
    wg_a, wu_a, wd_a = prep_ffn(inp["ffn1_w_gate"][l], inp["ffn1_w_up"][l], inp["ffn1_w_down"][l])
    wfm = w_in[:, fm_cols()]
    wfm_a = np.ascontiguousarray(wfm.reshape(8, 128, 15, 128).transpose(2, 1, 0, 3)).reshape(15, 128, 1024)
    wtm = w_in[:, tm_cols()]
    wtm_a = np.ascontiguousarray(wtm.reshape(8, 128, 664))
    g2 = lambda v: np.concatenate([v, v])
    gfm = np.ascontiguousarray(np.stack([g2(inp["nsa_q_norm"][l]), g2(inp["nsa_k_norm"][l]),
                                         g2(inp["swa_q_norm"][l]), g2(inp["swa_k_norm"][l])], axis=1))
    return dict(n1=vec8(inp["ffn1_norm"][l]), nm=vec8(inp["mix_norm"][l]), wg=wg_a, wu=wu_a, wd=wd_a,
                wfm=wfm_a, wtm=wtm_a, gfm=gfm)


def shard_xT(x):
    out = []
    for b in range(2):
        xb = x[b].reshape(16, 4, 128, D)
        for r in range(4):
            out.append(np.ascontiguousarray(xb[:, r].reshape(T, D).T))
    return out


def unshard_xT(parts):
    x = np.empty((2, 16, 4, 128, D), np.float32)
    for b in range(2):
        for r in range(4):
            x[b, :, r] = parts[b * 4 + r].T.reshape(16, 128, D)
    return x.reshape(2, S, D)


def _barrier(P):
    for eng in P.ENGS:
        waits = []
        for e2 in ("pe", "act", "dve", "pool"):
            if e2 != eng and P.cnt[e2] > 0:
                sk, v = ("c", e2), P.cnt[e2]
                if P.waited[eng].get(sk, 0) < v:
                    P.waited[eng][sk] = v
                    waits.append((sk, v))
        for q in P.ENGS:
            k = P.dcnt[q]
            for slot in range(min(k, NDMA_SLOTS)):
                n = (k - 1 - slot) // NDMA_SLOTS + 1
                sk = ("d", q, slot)
                if P.waited[eng].get(sk, 0) < 16 * n:
                    P.waited[eng][sk] = 16 * n
                    waits.append((sk, 16 * n))
        if waits:
            P.ops[eng].append((waits, None, None))


class Arena:
    def __init__(self, P, nbytes):
        self.P = P
        self.cap = nbytes
        self.t = P.sb("arena", [128, nbytes // 2], BF16)
        self.off = 0
        self.marks = []

    def alloc(self, shape, dt=BF16):
        n = int(np.prod(shape[1:]))
        esz = 4 if dt == F32 else 2
        nb = (n * esz + 63) // 64 * 64
        assert self.off + nb <= self.cap, ("arena overflow", self.off, nb)
        v = self.t[0:shape[0], self.off // 2:(self.off + nb) // 2]
        self.off += nb
        if dt == F32:
            v = v.bitcast(F32)
        v = v[:, 0:n]
        if len(shape) > 2:
            names = "abcde"[:len(shape) - 1]
            v = v.rearrange("p (" + " ".join(names) + ") -> p " + " ".join(names),
                            **{k: int(s) for k, s in zip(names, shape[1:])})
        return v

    def push(self):
        self.marks.append(self.off)

    def pop(self):
        _barrier(self.P)
        self.off = self.marks.pop()


def phase_sb(P, C, AR, d):
    al = AR.alloc
    pb = C.pb
    AR.push()
    KS = al([128, 8192])
    VS = al([128, 64, 128])
    QSp = [al([128, 2048]) for _ in range(2)]
    msb = al([128, 4, 128])
    et = [al([128, 512], F32) for _ in range(2)]
    spb = [al([128, 512]) for _ in range(2)]
    wb = [al([128, 512]) for _ in range(2)]
    cbf = [al([1, 512]) for _ in range(2)]
    P.dma("sp", lambda e: e.dma_start(out=msb, in_=d.msb), w=["msb"])
    for cc in range(2):
        P.dma("sp", lambda e, cc=cc: e.dma_start(out=KS, in_=d.kt[3 + cc]), w=["KS"])
        P.dma("sp", lambda e, cc=cc: e.dma_start(out=VS, in_=d.vsb[cc]), w=["VS"])
        for s_ in range(2):
            hp_ = slice(s_ * 64, s_ * 64 + 64)
            op_ = slice((1 - s_) * 64, (1 - s_) * 64 + 64)
            P.op("dve", lambda e, s_=s_, op_=op_: e.memset(QSp[s_][op_, :], 0.0), w=[("QSz", s_)])
            P.dma("sp", lambda e, s_=s_, hp_=hp_, cc=cc: e.dma_start(out=QSp[s_][hp_, :], in_=d.fmq[4 + cc, hp_, :]), w=[("QS", s_)])
        for sbk in range(4):
            nun = 16 * sbk + 16
            q0 = sbk * 512

            def info(u):
                kb = nun - 1 - u
                rel = kb - 16 * sbk
                if rel >= 0:
                    return kb, rel // 4, rel % 4, (rel % 4 == 3)
                return kb, 0, None, False

            def qk(s, u):
                kb, a, j, join = info(u)
                z = pb[s * 2 + u % 2]
                hp = slice(s * 64, s * 64 + 64)
                P.op("pe", lambda e: e.matmul(z[:, a * 128:512], lhsT=KS[:, kb * 128:(kb + 1) * 128],
                                              rhs=QSp[s][:, q0 + a * 128:q0 + 512], start=True, stop=False),
                     r=["KS", ("QS", s), ("QSz", s)], w=[("zb", s, u % 2)])

            for s in range(2):
                qk(s, 0)
            for u in range(nun):
                kb, a, j, join = info(u)
                c0 = a * 128
                cols = slice(c0, 512)
                newc = slice(c0, c0 + 128)
                oldc = slice(c0 + 128, 512) if join else cols
                has_old = (oldc.start < 512)
                for s in range(2):
                    z = pb[s * 2 + u % 2]
                    zt = ("zb", s, u % 2)
                    P.op("act", lambda e, z=z, s=s: e.activation(et[s][:, cols], z[:, cols], AF.Exp),
                         r=[zt], w=[("et", s)])
                    P.op("act", lambda e, s=s: e.activation(spb[s][:, cols], et[s][:, cols], AF.Ln, bias=1.0),
                         r=[("et", s)], w=[("spb", s)])
                    if j is not None:
                        P.op("dve", lambda e, s=s: e.tensor_tensor(out=spb[s][:, newc], in0=spb[s][:, newc],
                                                                   in1=msb[:, j, :], op=ALU.mult),
                             r=[("spb", s), "msb"], w=[("spb", s)])
                    P.op("pe", lambda e, z=z, s=s: e.matmul(z[:, cols], lhsT=C.nu, rhs=spb[s][:, cols],
                                                            start=False, stop=(not has_old)),
                         r=[("spb", s), "nu"], w=[zt])
                    if has_old:
                        P.op("pe", lambda e, z=z, s=s: e.matmul(z[:, oldc], lhsT=C.negone, rhs=cbf[s][0:1, oldc],
                                                                start=False, stop=True),
                             r=[("cbf", s), "negone"], w=[zt])
                    cb = pb[4 + s]
                    P.op("pe", lambda e, cb=cb, s=s: e.matmul(cb[0:1, cols], lhsT=C.ones[:, 0:1], rhs=spb[s][:, cols],
                                                              start=(u == 0), stop=True),
                         r=[("spb", s), "ones"], w=[("cb", s)])
                    if u + 1 < nun:
                        P.op("dve", lambda e, cb=cb, s=s: e.tensor_copy(cbf[s][0:1, cols], cb[0:1, cols]),
                             r=[("cb", s)], w=[("cbf", s)])
                        qk(s, u + 1)
                for s in range(2):
                    z = pb[s * 2 + u % 2]
                    zt = ("zb", s, u % 2)
                    P.op("act", lambda e, z=z, s=s: e.activation(wb[s][:, cols], z[:, cols], AF.Exp),
                         r=[zt], w=[("wb", s)])
                    if j is not None:
                        P.op("dve", lambda e, s=s: e.tensor_tensor(out=wb[s][:, newc], in0=wb[s][:, newc],
                                                                   in1=msb[:, j, :], op=ALU.mult),
                             r=[("wb", s), "msb"], w=[("wb", s)])
                    ob = pb[6]
                    hp = slice(s * 64, s * 64 + 64)
                    last = (kb == 0)
                    P.op("pe", lambda e, s=s, hp=hp: e.matmul(ob[hp, cols], lhsT=VS[:, kb, hp], rhs=wb[s][:, cols],
                                                              start=(u == 0), stop=last),
                         r=[("wb", s), "VS"], w=[("ob", s)])
            P.op("act", lambda e, cc=cc, q0=q0: e.copy(C.ysbT[:, cc, q0:q0 + 512], pb[6][:, :]),
                 r=[("ob", 0), ("ob", 1)], w=[("ysb", cc, sbk)])
    AR.pop()


def phase_swa(P, C, AR, d):
    al = AR.alloc
    pb = C.pb
    AR.push()
    KW = al([128, 16, 256])
    VW = al([128, 16, 2, 2, 65])
    QWp = [al([128, 2, 2048]) for _ in range(2)]
    BSW = al([128, 2, 2, 256], F32)
    esink = al([128, 4], F32)
    tmp = [al([128, 256], F32) for _ in range(2)]
    pw = [al([128, 256]) for _ in range(2)]
    den = al([128, 4], F32)
    rden = al([128, 4], F32)
    ytm = [al([128, 256]) for _ in range(2)]
    P.dma("sp", lambda e: e.dma_start(out=KW, in_=d.kw2), w=["KW"])
    P.dma("sp", lambda e: e.dma_start(out=VW.rearrange("p a b c d -> p (a b c d)"), in_=d.vw2), w=["VW"])
    for g_ in range(2):
        gp_ = slice(g_ * 64, g_ * 64 + 64)
        op_ = slice((1 - g_) * 64, (1 - g_) * 64 + 64)
        P.op("dve", lambda e, g_=g_, op_=op_: e.memset(QWp[g_][op_, :, :].rearrange("p c t -> p (c t)"), 0.0), w=[("QWz", g_)])
        for c in range(2):
            P.dma("sp", lambda e, c=c, g_=g_, gp_=gp_: e.dma_start(out=QWp[g_][gp_, c, :], in_=d.fmq[6 + c, gp_, :]), w=[("QW", g_, c)])
    P.dma("sp", lambda e: e.dma_start(out=BSW.rearrange("p a b c -> p (a b c)"), in_=d.bsw), w=["BSW"])
    P.dma("sp", lambda e: e.dma_start(out=esink, in_=d.sinks), w=["esink"])
    P.op("act", lambda e: e.activation(esink, esink, AF.Exp), r=["esink"], w=["esink"])
    k = 0
    ob = pb[4]
    for i in range(16):
        qs = _sl(i, 128)
        for g in range(2):
            gp = slice(g * 64, g * 64 + 64)
            for di in range(2):
                s = k % 2
                k += 1
                z = pb[s]
                P.op("pe", lambda e, z=z, g=g, di=di: e.matmul(z[:, 0:256], lhsT=KW[:, i, di * 128:(di + 1) * 128],
                                                               rhs=QWp[g][:, :, qs], start=True, stop=True),
                     r=["KW", ("QW", g, 0), ("QW", g, 1), ("QWz", g)], w=[("z", s)])
                P.op("dve", lambda e, z=z, s=s, di=di, g=g: e.tensor_tensor(out=tmp[s], in0=z[:, 0:256],
                                                                            in1=BSW[:, di, g, :], op=ALU.add),
                     r=[("z", s), "BSW"], w=[("tmp", s)])
                P.op("act", lambda e, s=s: e.activation(pw[s], tmp[s], AF.Exp), r=[("tmp", s)], w=[("pw", s)])
                for r in range(2):
                    hcol = (g * 2 + r) * 65
                    P.op("pe", lambda e, s=s, r=r, hcol=hcol, di=di, g=g: e.matmul(
                        ob[:, hcol:hcol + 65], lhsT=pw[s][:, r * 128:(r + 1) * 128], rhs=VW[:, i, di, g, :],
                        start=(di == 0 and r == 0), stop=(di == 1)), r=[("pw", s), "VW"], w=["ob_w"])
        P.op("dve", lambda e: e.tensor_tensor(out=den, in0=ob[:, 64:260:65], in1=esink, op=ALU.add),
             r=["ob_w", "esink"], w=["den"])
        P.op("dve", lambda e: e.reciprocal(rden, den), r=["den"], w=["rden"])
        ys = i % 2
        for h in range(4):
            P.op("dve", lambda e, h=h, ys=ys: e.tensor_scalar(ytm[ys][:, h * 64:(h + 1) * 64], ob[:, h * 65:h * 65 + 64],
                                                              rden[:, h:h + 1], None, op0=ALU.mult),
                 r=["ob_w", "rden"], w=[("ytm", ys)])
        for c in range(2):
            P.op("pe", lambda e, c=c, ys=ys: e.transpose(C.ptb[:, c * 128:(c + 1) * 128], ytm[ys][:, c * 128:(c + 1) * 128],
                                                          C.ident), r=[("ytm", ys), "ident"], w=["ptb"])
        P.op("act", lambda e, qs=qs: e.copy(C.yswT[:, :, qs], C.ptb[:, 0:256].rearrange("p (c q) -> p c q", c=2)),
             r=["ptb"], w=[("ysw", i)])
    AR.pop()


def phase_nsa(P, C, AR, d):
    al = AR.alloc
    pb = C.pb
    AR.push()
    KC = al([128, 512])
    RC = al([128, 4, 2, 193])
    AR.push()
    CK = al([128, 8192])
    CV = al([128, 8192])
    W1 = [al([128, 32, 256]) for _ in range(2)]
    W2 = [al([128, 2, 64]) for _ in range(2)]
    POST = al([128, 32])
    hid = al([128, 2, 512])
    cpos = al([128, 4], F32)
    sqc = al([128, 512])
    lnt = al([128, 512], F32)
    rst = al([128, 512], F32)
    gk = al([128, 1], F32)
    P.dma("sp", lambda e: e.dma_start(out=CK, in_=d.kt[0]), w=["CK"])
    P.dma("sp", lambda e: e.dma_start(out=CV, in_=d.kt[1]), w=["CV"])
    P.dma("sp", lambda e: e.dma_start(out=gk, in_=d.gk), w=["gk"])
    for wi in range(2):
        P.dma("pool", lambda e, wi=wi: e.dma_start(out=W1[wi].rearrange("p j h -> p (j h)"), in_=d.w1[wi]), w=[("W1", wi)])
        P.dma("pool", lambda e, wi=wi: e.dma_start(out=W2[wi].rearrange("p c h -> p (c h)"), in_=d.w2[wi]), w=[("W2", wi)])
    P.dma("pool", lambda e: e.dma_start(out=POST, in_=d.post), w=["POST"])
    for g in range(2):
        P.dma("sp", lambda e, g=g: e.dma_start(out=RC[:, :, g, 64:193], in_=d.ovl), w=[("RCo", g)])
    P.op("dve", lambda e: e.memset(hid[:, :, 511:512], 0.0), w=["hidpad"])
    kk = 0
    SRC = [CK, CV]
    for wi in range(2):
        for hc in range(2):
            for j in range(32):
                P.op("pe", lambda e, wi=wi, hc=hc, j=j: e.matmul(pb[5][:, 0:1], lhsT=W1[wi][0:64, j, hc * 128:(hc + 1) * 128],
                                                                 rhs=POST[0:64, j:j + 1], start=(j == 0), stop=(j == 31)),
                     r=[("W1", wi), "POST"], w=["pcp"])
            P.op("act", lambda e, wi=wi, hc=hc: e.copy(cpos[:, wi * 2 + hc:wi * 2 + hc + 1], pb[5][:, 0:1]),
                 r=["pcp"], w=[("cpos", wi, hc)])
        for g in range(2):
            gp = slice(g * 64, g * 64 + 64)
            for hc in range(2):
                z = pb[kk % 2]
                zt = ("z", kk % 2)
                kk += 1
                for j in range(32):
                    P.op("pe", lambda e, z=z, wi=wi, hc=hc, j=j, gp=gp: e.matmul(
                        z[:, 0:511], lhsT=W1[wi][gp, j, hc * 128:(hc + 1) * 128], rhs=SRC[wi][gp, j:j + 16 * 510 + 1:16],
                        start=(j == 0), stop=(j == 31)), r=[("W1", wi), ["CK", "CV"][wi]], w=[zt])
                P.op("act", lambda e, z=z, wi=wi, hc=hc: e.activation(hid[:, hc, 0:511], z[:, 0:511], AF.Silu,
                                                                       bias=cpos[:, wi * 2 + hc:wi * 2 + hc + 1]),
                     r=[zt, ("cpos", wi, hc), "hidpad"], w=[("hid", hc)])
            if wi == 0:
                for hc in range(2):
                    P.op("pe", lambda e, hc=hc, gp=gp: e.matmul(pb[2][gp, 0:512], lhsT=W2[0][:, hc, :], rhs=hid[:, hc, :],
                                                                start=(hc == 0), stop=(hc == 1)),
                         r=[("hid", hc), ("W2", 0)], w=[("kcp", g)])
            else:
                for ct in range(4):
                    for hc in range(2):
                        P.op("pe", lambda e, hc=hc, ct=ct: e.matmul(pb[3][:, ct * 64:(ct + 1) * 64],
                                                                    lhsT=hid[:, hc, ct * 128:(ct + 1) * 128], rhs=W2[1][:, hc, :],
                                                                    start=(hc == 0), stop=(hc == 1)),
                             r=[("hid", hc), ("W2", 1)], w=["vcp"])
                P.op("dve", lambda e, g=g: e.tensor_copy(RC[:, :, g, 0:64], pb[3][:, 0:256].rearrange("p (c d) -> p c d", c=4)),
                     r=["vcp"], w=[("RCv", g)])
        if wi == 0:
            P.op("act", lambda e: e.activation(sqc, pb[2][:, :], AF.Square), r=[("kcp", 0), ("kcp", 1)], w=["sqc"])
            P.op("pe", lambda e: e.matmul(pb[3][:, :], lhsT=C.bd, rhs=sqc, start=True, stop=True), r=["sqc", "bd"], w=["vcp"])
            P.op("act", lambda e: e.activation(lnt, pb[3][:, :], AF.Ln, bias=EPS, scale=1.0 / 64), r=["vcp"], w=["lntc"])
            P.op("act", lambda e: e.activation(rst, lnt, AF.Exp, scale=-0.5), r=["lntc"], w=["rstc"])
            P.op("dve", lambda e: e.scalar_tensor_tensor(out=KC, in0=pb[2][:, :], scalar=gk[:, 0:1], in1=rst,
                                                         op0=ALU.mult, op1=ALU.mult),
                 r=[("kcp", 0), ("kcp", 1), "rstc", "gk"], w=["KC"])
    AR.pop()
    QNp = [al([128, 4, 2048]) for _ in range(2)]
    KSL = al([128, 8192])
    VSL = al([128, 64, 2, 65])
    FM = al([128, 8192])
    BS = al([128, 2, 5, 512], F32)
    BW = al([128, 2, 3, 512], F32)
    BC = al([32, 2, 512], F32)
    SEL = al([32, 16, 2, 128], F32)
    AT = al([128, 248], F32)
    CT = al([128, 248], F32)
    b31 = al([128, 8], F32)
    gsig = al([128, 16, 24], F32)
    kwn = [al([128, 640]) for _ in range(2)]
    vwn = [al([128, 5, 2, 65]) for _ in range(2)]
    ex = [al([128, 512]) for _ in range(2)]
    tmpf = [al([128, 512], F32) for _ in range(2)]
    yacc = al([128, 512], F32)
    ybf = al([128, 512])
    m01 = al([128, 128])
    m4 = [al([128, 4, 128]) for _ in range(2)]
    score = al([128, 128], F32)
    scr2 = al([128, 128], F32)
    mx = al([128, 16], F32)
    zr = al([128, 4], F32)
    rz = al([128, 4], F32)
    coef = al([128, 4], F32)
    for g_ in range(2):
        gp_ = slice(g_ * 64, g_ * 64 + 64)
        op_ = slice((1 - g_) * 64, (1 - g_) * 64 + 64)
        P.op("dve", lambda e, g_=g_, op_=op_: e.memset(QNp[g_][op_, :, :].rearrange("p c t -> p (c t)"), 0.0), w=[("QNz", g_)])
        for c in range(4):
            P.dma("sp", lambda e, c=c, g_=g_, gp_=gp_: e.dma_start(out=QNp[g_][gp_, c, :], in_=d.fmq[c, gp_, :]), w=[("QN", g_, c)])
    P.dma("sp", lambda e: e.dma_start(out=KSL, in_=d.kt[2]), w=["KSL"])
    P.dma("sp", lambda e: e.dma_start(out=VSL.rearrange("p t g c -> p (t g c)"), in_=d.vsl), w=["VSL"])
    P.dma("sp", lambda e: e.dma_start(out=FM, in_=d.fmask), w=["FM"])
    P.dma("sp", lambda e: e.dma_start(out=BS.rearrange("p a b c -> p (a b c)"), in_=d.bs), w=["BS"])
    P.dma("sp", lambda e: e.dma_start(out=BW.rearrange("p a b c -> p (a b c)"), in_=d.bw), w=["BW"])
    P.dma("sp", lambda e: e.dma_start(out=BC.rearrange("p a c -> p (a c)"), in_=d.bc), w=["BC"])
    P.dma("sp", lambda e: e.dma_start(out=SEL.rearrange("p a b c -> p (a b c)"), in_=d.sel), w=["SEL"])
    P.dma("sp", lambda e: e.dma_start(out=AT, in_=d.at), w=["AT"])
    P.dma("sp", lambda e: e.dma_start(out=CT, in_=d.ct), w=["CT"])
    P.dma("sp", lambda e: e.dma_start(out=b31, in_=d.b31), w=["b31"])
    P.dma("sp", lambda e: e.dma_start(out=gsig.rearrange("p a c -> p (a c)"), in_=d.tmg), w=["gsig"])
    gs2 = gsig.rearrange("p a c -> p (a c)")
    P.op("act", lambda e: e.activation(gs2, gs2, AF.Exp, scale=-1.0), r=["gsig"], w=["gsig"])
    P.op("dve", lambda e: e.tensor_scalar(gs2, gs2, 1.0, None, op0=ALU.add), r=["gsig"], w=["gsig"])
    P.op("dve", lambda e: e.reciprocal(gs2, gs2), r=["gsig"], w=["gsig"])
    for g in range(2):
        for r in range(4):
            col = b31[:, g * 4 + r:g * 4 + r + 1]
            rs = slice(r * 128, (r + 1) * 128)
            P.op("dve", lambda e, g=g, rs=rs, col=col: e.tensor_scalar(BS[:, g, :, rs], BS[:, g, :, rs], col, None,
                                                                       op0=ALU.subtract), r=["BS", "b31"], w=["BS"])
            P.op("dve", lambda e, g=g, rs=rs, col=col: e.tensor_scalar(BW[:, g, :, rs], BW[:, g, :, rs], col, None,
                                                                       op0=ALU.subtract), r=["BW", "b31"], w=["BW"])
            P.op("dve", lambda e, g=g, rs=rs, r=r: e.tensor_scalar(BC[:, g, rs], BC[:, g, rs],
                                                                   b31[0:32, g * 4 + r:g * 4 + r + 1], None,
                                                                   op0=ALU.subtract), r=["BC", "b31"], w=["BC"])
    zbank = [pb[0], pb[1]]
    otr = pb[6]
    oT_sb = al([65, 512], F32)
    ex = ex + [al([128, 512])]
    yaccs = [yacc, al([128, 512], F32)]
    m4s = [[m4[0], m4[1]], [al([128, 4, 128]), al([128, 4, 128])]]
    zr2 = [al([128, 4], F32) for _ in range(3)]
    rz2 = [al([128, 4], F32) for _ in range(3)]
    cf2 = [al([128, 4], F32) for _ in range(3)]
    units = []
    oc = [pb[2], pb[3]]
    osl = pb[4]
    owin = pb[5]
    qtoks = [[("QN", g_, c) for c in range(4)] + [("QNz", g_)] for g_ in range(2)]

    def mk_unit(s1, s3, add=None, after=None):
        units.append(dict(s1=s1, s3=s3, add=add, after=after))

    def add_cmp(i, g):
        qs = _sl(i, 128)
        gp = slice(g * 64, g * 64 + 64)
        qrhs = QNp[g][:, :, qs]
        qtok = qtoks[g]
        ctmax = i // 4
        ya = yaccs[i % 2]
        for ct in range(ctmax + 1):
            slot = ct - (ctmax - 1)
            hasb = slot >= 0

            def s1(z, zt, ct=ct, slot=slot, hasb=hasb):
                P.op("pe", lambda e: e.matmul(z[:, :], lhsT=KC[:, ct * 128:(ct + 1) * 128], rhs=qrhs, start=True,
                                              stop=(not hasb)), r=["KC"] + qtok, w=[zt])
                if hasb:
                    P.op("pe", lambda e: e.matmul(z[:, :], lhsT=SEL[:, i, slot, :], rhs=BC[:, g, :], start=False, stop=True),
                         r=["SEL", "BC"], w=[zt])

            def s3(exb, et, ct=ct):
                for r in range(4):
                    P.op("pe", lambda e, r=r: e.matmul(
                        oc[r // 2][:, (r % 2) * 193:(r % 2) * 193 + 193], lhsT=exb[:, r * 128:(r + 1) * 128],
                        rhs=RC[:, ct, g, :], start=(ct == 0 and r % 2 == 0), stop=(ct == ctmax)),
                         r=[et, ("RCo", g), ("RCv", g)], w=["oc"])

            def after_topk():
                zr, rz, coef = zr2[0], rz2[0], cf2[0]
                for b in range(2):
                    P.op("dve", lambda e, b=b: e.tensor_scalar(zr[:, 2 * b:2 * b + 2], oc[b][:, 64:64 + 386:193], 1e-30, None,
                                                               op0=ALU.add), r=["oc"], w=[("zr0", b)])
                P.op("dve", lambda e: e.reciprocal(rz, zr), r=[("zr0", 0), ("zr0", 1)], w=["rz0"])
                P.op("dve", lambda e: e.tensor_tensor(out=coef, in0=rz, in1=gsig[:, i, g * 4:g * 4 + 4], op=ALU.mult),
                     r=["rz0", "gsig"], w=["coef0"])
                for r in range(4):
                    src = oc[r // 2][:, (r % 2) * 193 + 65:(r % 2) * 193 + 193]
                    if r == 0:
                        P.op("dve", lambda e, src=src: e.tensor_scalar(score, src, rz[:, 0:1], None, op0=ALU.mult),
                             r=["oc", "rz0"], w=["score"])
                    else:
                        P.op("dve", lambda e, src=src, r=r: e.scalar_tensor_tensor(out=score, in0=src, scalar=rz[:, r:r + 1],
                                                                                   in1=score, op0=ALU.mult, op1=ALU.add),
                             r=["oc", "rz0", "score"], w=["score"])
                    hs = slice((g * 4 + r) * 64, (g * 4 + r) * 64 + 64)
                    osrc = oc[r // 2][:, (r % 2) * 193:(r % 2) * 193 + 64]
                    P.op("dve", lambda e, osrc=osrc, r=r, hs=hs: e.tensor_scalar(ya[:, hs], osrc, coef[:, r:r + 1], None,
                                                                                 op0=ALU.mult),
                         r=["oc", "coef0"], w=[("yacc", i % 2, g, r)])
                u0 = 120 - 8 * i
                P.op("dve", lambda e: e.tensor_tensor(out=score, in0=score, in1=AT[:, u0:u0 + 128], op=ALU.mult),
                     r=["score", "AT"], w=["score"])
                P.op("dve", lambda e: e.tensor_tensor(out=score, in0=score, in1=CT[:, u0:u0 + 128], op=ALU.add),
                     r=["score", "CT"], w=["score"])
                P.op("dve", lambda e: e.memset(score[:, 0:1], 1e4), r=["score"], w=["score"])
                P.op("dve", lambda e: e.max(out=mx[:, 0:8], in_=score), r=["score"], w=["mx0"])
                P.op("dve", lambda e: e.match_replace(out=scr2, in_to_replace=mx[:, 0:8], in_values=score, imm_value=-1e30),
                     r=["score", "mx0"], w=["scr2"])
                P.op("dve", lambda e: e.max(out=mx[:, 8:16], in_=scr2), r=["scr2"], w=["mx1"])
                P.op("dve", lambda e: e.tensor_scalar(m01, score, mx[:, 15:16], None, op0=ALU.is_lt),
                     r=["score", "mx1"], w=["m01"])
                P.op("pe", lambda e: e.transpose(C.ptb[:, 0:128], m01, C.ident), r=["m01", "ident"], w=["ptb"])
                P.op("dve", lambda e: e.tensor_copy(
                    m4s[i % 2][g], C.ptb[:, 0:128].rearrange("p (o q) -> p o q", o=1).to_broadcast([128, 4, 128])),
                     r=["ptb"], w=[("m4", i % 2, g)])

            mk_unit(s1, s3, None, after_topk if ct == ctmax else None)

    def fin_branch(i, g, acc, tok, br):
        ya = yaccs[i % 2]
        zr, rz, coef = zr2[br], rz2[br], cf2[br]
        P.op("dve", lambda e: e.tensor_scalar(zr, acc[:, 64:260:65], 1e-30, None, op0=ALU.add), r=[tok], w=[("zr", br)])
        P.op("dve", lambda e: e.reciprocal(rz, zr), r=[("zr", br)], w=[("rz", br)])
        gc = br * 8 + g * 4
        P.op("dve", lambda e: e.tensor_tensor(out=coef, in0=rz, in1=gsig[:, i, gc:gc + 4], op=ALU.mult),
             r=[("rz", br), "gsig"], w=[("coef", br)])
        for r in range(4):
            hs = slice((g * 4 + r) * 64, (g * 4 + r) * 64 + 64)
            P.op("dve", lambda e, r=r, hs=hs: e.scalar_tensor_tensor(
                out=ya[:, hs], in0=acc[:, r * 65:r * 65 + 64], scalar=coef[:, r:r + 1], in1=ya[:, hs],
                op0=ALU.mult, op1=ALU.add), r=[tok, ("coef", br), ("yacc", i % 2, g, r)], w=[("yacc", i % 2, g, r)])

    def fin_fm(i, g, acc, tok, br):
        P.op("act", lambda e: e.copy(oT_sb, acc[0:65, :]), r=[tok], w=["oT_sb"])
        for r in range(4):
            P.op("pe", lambda e, r=r: e.transpose(otr[:, r * 65:(r + 1) * 65], oT_sb[:, r * 128:(r + 1) * 128], C.identf[0:65, 0:65]),
                 r=["oT_sb", "identf"], w=["otr"])
        fin_branch(i, g, otr, "otr", br)

    def tile_fin(i):
        qs = _sl(i, 128)
        ya = yaccs[i % 2]
        P.op("act", lambda e: e.copy(ybf, ya), r=[("yacc", i % 2, g, r) for g in range(2) for r in range(4)], w=["ybf"])
        for c in range(4):
            P.op("pe", lambda e, c=c: e.transpose(C.ptb[:, 128 + c * 128:256 + c * 128], ybf[:, c * 128:(c + 1) * 128], C.ident),
                 r=["ybf", "ident"], w=["ptb2"])
        P.op("act", lambda e: e.copy(C.ynsT[:, :, qs], C.ptb[:, 128:640].rearrange("p (c q) -> p c q", c=4)),
             r=["ptb2"], w=[("yns", i)])

    def add_slc(i, g):
        qs = _sl(i, 128)
        gp = slice(g * 64, g * 64 + 64)
        qrhs = QNp[g][:, :, qs]
        qtok = qtoks[g]
        m4f = m4s[i % 2][g].rearrange("p r q -> p (r q)")
        nkb = 4 * i + 4
        for kb in range(nkb):
            jj = kb - (4 * i - 1)

            def s1(z, zt, kb=kb):
                P.op("pe", lambda e: e.matmul(z[:, :], lhsT=KSL[:, kb * 128:(kb + 1) * 128], rhs=qrhs, start=True, stop=False),
                     r=["KSL"] + qtok, w=[zt])
                P.op("pe", lambda e: e.matmul(z[:, :], lhsT=FM[:, kb * 128:(kb + 1) * 128], rhs=m4f, start=False, stop=True),
                     r=["FM", ("m4", i % 2, g)], w=[zt])

            def s3(exb, et, kb=kb):
                P.op("pe", lambda e: e.matmul(osl[0:65, :], lhsT=VSL[:, kb, g, :], rhs=exb, start=(kb == 0), stop=(kb == nkb - 1)),
                     r=[et, "VSL"], w=["osl"])

            add = (BS[:, g, jj, :], "BS") if jj >= 0 else None
            after = (lambda: fin_fm(i, g, osl, "osl", 1)) if kb == nkb - 1 else None
            mk_unit(s1, s3, add, after)

    def add_win(i, g):
        qs = _sl(i, 128)
        ws = i % 2
        gp = slice(g * 64, g * 64 + 64)
        qrhs = QNp[g][:, :, qs]
        qtok = qtoks[g]
        for di in range(5):
            bidx = {0: 0, 3: 1, 4: 2}.get(di)

            def s1(z, zt, di=di):
                P.op("pe", lambda e: e.matmul(z[:, :], lhsT=kwn[ws][:, di * 128:(di + 1) * 128], rhs=qrhs, start=True, stop=True),
                     r=[("kwn", ws)] + qtok, w=[zt])

            def s3(exb, et, di=di):
                P.op("pe", lambda e: e.matmul(owin[0:65, :], lhsT=vwn[ws][:, di, g, :], rhs=exb, start=(di == 0), stop=(di == 4)),
                     r=[et, ("vwn", ws)], w=["owin"])

            add = (BW[:, g, bidx, :], "BW") if bidx is not None else None
            if di == 4:
                def after(i=i, g=g):
                    fin_fm(i, g, owin, "owin", 2)
                    if g == 1:
                        tile_fin(i)
            else:
                after = None
            mk_unit(s1, s3, add, after)

    def load_win(i):
        ws = i % 2
        P.dma("sp", lambda e: e.dma_start(out=kwn[ws], in_=d.kwn[i]), w=[("kwn", ws)])
        P.dma("sp", lambda e: e.dma_start(out=vwn[ws].rearrange("p a b c -> p (a b c)"), in_=d.vwn[i]), w=[("vwn", ws)])

    marks = {}
    add_cmp(0, 0)
    add_cmp(0, 1)
    for i in range(16):
        marks[len(units)] = i
        if i + 1 < 16:
            add_cmp(i + 1, 0)
            add_cmp(i + 1, 1)
        for g in range(2):
            add_slc(i, g)
            add_win(i, g)
    nun = len(units)

    def do_s12(u):
        un = units[u]
        z = zbank[u % 2]
        zt = ("z", u % 2)
        un["s1"](z, zt)
        exb = ex[u % 3]
        et = ("ex", u % 3)
        if un["add"] is not None:
            tab, ttok = un["add"]
            tf = tmpf[u % 2]
            P.op("dve", lambda e: e.tensor_tensor(out=tf, in0=z[:, :], in1=tab, op=ALU.add), r=[zt, ttok], w=[("tmpf", u % 2)])
            P.op("act", lambda e: e.activation(exb, tf, AF.Exp), r=[("tmpf", u % 2)], w=[et])
        else:
            P.op("act", lambda e: e.activation(exb, z[:, :], AF.Exp), r=[zt], w=[et])

    def do_s3(u):
        un = units[u]
        un["s3"](ex[u % 3], ("ex", u % 3))
        if un["after"] is not None:
            un["after"]()

    for u in range(nun):
        if u in marks:
            load_win(marks[u])
        do_s12(u)
        if u >= 1:
            do_s3(u - 1)
    do_s3(nun - 1)
    AR.pop()


def phase_merge_ffn2(P, C, AR, d):
    al = AR.alloc
    pb = C.pb
    AR.push()
    x = al([128, 8, T], F32)
    h = al([128, 8, T])
    gn = al([128, 16], F32)
    C.sq = al([128, 8, 512])
    C.lnt = al([128, 512], F32)
    C.rstd = al([128, 512], F32)
    C.wgb = [al([128, 8, 128]) for _ in range(2)]
    C.wub = [al([128, 8, 128]) for _ in range(2)]
    for c in range(8):
        P.dma("sp", lambda e, c=c: e.dma_start(out=x[:, c, :], in_=d.x1T[_sl(c, 128), :]),
              w=[("x", c, tb) for tb in range(4)])
    P.dma("sp", lambda e: e.dma_start(out=gn[:, 0:8], in_=d.nm), w=["gains"])
    P.dma("sp", lambda e: e.dma_start(out=gn[:, 8:16], in_=d.n2), r=["gains"], w=["gains"])
    rmsnorm_h(P, C, x, h, gn[:, 0:8])
    AR.push()
    mg = al([128, 8, T])
    wbg = [[al([128, 8, 128]) for _ in range(3)] for _ in range(2)]
    wup = [[al([128, 4, 128]), al([128, 2, 128]), al([128, 2, 128])] for _ in range(2)]
    sgt = [al([128, 512]) for _ in range(2)]
    ysrc = [(C.ynsT, 4, "yns"), (C.ysbT, 2, "ysb"), (C.yswT, 2, "ysw")]
    ytoks = ([("yns", i) for i in range(16)] + [("ysb", cc, sbk) for cc in range(2) for sbk in range(4)]
             + [("ysw", i) for i in range(16)])
    k = 0
    for dc in range(8):
        s = dc % 2
        for br in range(3):
            P.dma("pool", lambda e, s=s, br=br, dc=dc: e.dma_start(out=wbg[s][br].rearrange("p c j -> p (c j)"),
                                                                   in_=d.wbg[dc, br]), w=[("wbg", s, br)])
            P.dma("pool", lambda e, s=s, br=br, dc=dc: e.dma_start(out=wup[s][br].rearrange("p c j -> p (c j)"),
                                                                   in_=d.wup[br][dc]), w=[("wup", s, br)])
        for tb in range(4):
            ts = _sl(tb, 512)
            for br in range(3):
                ysb_, nkc, _ = ysrc[br]
                pgt = k % 2
                pup = 2 + k % 2
                si = k % 2
                k += 1
                for c in range(8):
                    P.op("pe", lambda e, c=c, s=s, br=br, pgt=pgt, ts=ts: e.matmul(
                        pb[pgt][:, :], lhsT=wbg[s][br][:, c, :], rhs=h[:, c, ts], start=(c == 0), stop=(c == 7)),
                         r=[("wbg", s, br), ("h", c, tb)], w=[("pb", pgt)])
                for c in range(nkc):
                    P.op("pe", lambda e, c=c, s=s, br=br, pup=pup, ts=ts, ysb_=ysb_, nkc=nkc: e.matmul(
                        pb[pup][:, :], lhsT=wup[s][br][:, c, :], rhs=ysb_[:, c, ts], start=(c == 0), stop=(c == nkc - 1)),
                         r=[("wup", s, br)] + ytoks, w=[("pb", pup)])
                P.op("act", lambda e, pgt=pgt, si=si: e.activation(sgt[si], pb[pgt][:, :], AF.Sigmoid),
                     r=[("pb", pgt)], w=[("sgt", si)])
                if br == 0:
                    P.op("dve", lambda e, pup=pup, si=si, dc=dc, ts=ts: e.tensor_tensor(
                        out=mg[:, dc, ts], in0=sgt[si], in1=pb[pup][:, :], op=ALU.mult),
                         r=[("sgt", si), ("pb", pup)], w=[("mg", dc, tb)])
                else:
                    P.op("dve", lambda e, pup=pup, si=si: e.tensor_tensor(
                        out=sgt[si], in0=sgt[si], in1=pb[pup][:, :], op=ALU.mult),
                         r=[("sgt", si), ("pb", pup)], w=[("sgt", si)])
                    P.op("dve", lambda e, si=si, dc=dc, ts=ts: e.tensor_tensor(
                        out=mg[:, dc, ts], in0=mg[:, dc, ts], in1=sgt[si], op=ALU.add),
                         r=[("sgt", si), ("mg", dc, tb)], w=[("mg", dc, tb)])
    k = 0
    for dc2 in range(8):
        s = dc2 % 2
        P.dma("pool", lambda e, s=s, dc2=dc2: e.dma_start(out=C.wgb[s].rearrange("p c j -> p (c j)"), in_=d.wo[dc2]),
              w=[("wg", s)])
        for tb in range(4):
            ts = _sl(tb, 512)
            po = 4 + k % 2
            k += 1
            for c in range(8):
                P.op("pe", lambda e, c=c, s=s, po=po, ts=ts: e.matmul(pb[po][:, :], lhsT=C.wgb[s][:, c, :], rhs=mg[:, c, ts],
                                                                      start=(c == 0), stop=(c == 7)),
                     r=[("wg", s), ("mg", c, tb)], w=[("pb", po)])
            P.op("dve", lambda e, dc2=dc2, ts=ts, po=po: e.tensor_tensor(out=x[:, dc2, ts], in0=x[:, dc2, ts],
                                                                         in1=pb[po][:, :], op=ALU.add),
                 r=[("pb", po), ("x", dc2, tb)], w=[("x", dc2, tb)])
    AR.pop()
    C.wdb = [al([128, 11, 128]) for _ in range(2)]
    C.sg = [al([128, 512]) for _ in range(2)]
    act = al([128, 11, T])
    rmsnorm_h(P, C, x, h, gn[:, 8:16])
    ffn(P, C, x, h, act, d.wg, d.wu, d.wd)
    for c in range(8):
        P.dma("sp", lambda e, c=c: e.dma_start(out=d.x2T[_sl(c, 128), :], in_=x[:, c, :]),
              r=[("x", c, tb) for tb in range(4)])
    AR.pop()


def build_B(phases=("sb", "swa", "nsa", "merge")):
    nc = bass.Bass("TRN2", target_bir_lowering=False)
    dt = lambda n, s, t=F32, k="ExternalInput": nc.dram_tensor(n, s, t, kind=k).ap()
    d = NS()
    d.x1T = dt("x1T", [D, T])
    d.fmq = dt("fmq", [8, 128, T], BF16)
    d.tmg = dt("tmg", [128, 16 * 24])
    d.kt = dt("kt", [5, 128, S], BF16)
    d.vsb = dt("vsb", [2, 128, 64 * 128], BF16)
    d.vsl = dt("vsl", [128, 64 * 130], BF16)
    d.kwn = dt("kwn", [16, 128, 640], BF16)
    d.vwn = dt("vwn", [16, 128, 650], BF16)
    d.kw2 = dt("kw2", [128, 16, 256], BF16)
    d.vw2 = dt("vw2", [128, 16 * 260], BF16)
    d.msb = dt("msb", [128, 4, 128], BF16)
    d.bs = dt("bs", [128, 2 * 5 * 512])
    d.bw = dt("bw", [128, 2 * 3 * 512])
    d.bsw = dt("bsw", [128, 2 * 2 * 256])
    d.bc = dt("bc", [32, 2 * 512])
    d.sel = dt("sel", [32, 16 * 2 * 128])
    d.at = dt("at", [128, 248])
    d.ct = dt("ct", [128, 248])
    d.fmask = dt("fmask", [128, S], BF16)
    d.ovl = dt("ovl", [128, 4, 129], BF16)
    d.b31 = dt("b31", [128, 8])
    d.sinks = dt("sinks", [128, 4])
    d.w1 = [dt("w1k", [128, 32 * 256]), dt("w1v", [128, 32 * 256])]
    d.w2 = [dt("w2k", [128, 128]), dt("w2v", [128, 128])]
    d.post = dt("post", [128, 32])
    d.gk = dt("gk", [128, 1])
    d.nm = dt("nm", [128, 8])
    d.n2 = dt("n2", [128, 8])
    d.wbg = dt("wbg", [8, 3, 128, 1024])
    d.wup = [dt("wupn", [8, 128, 512]), dt("wups", [8, 128, 256]), dt("wupw", [8, 128, 256])]
    d.wo = dt("wo", [8, 128, 1024])
    d.wg = dt("wg", [22, 128, 1024])
    d.wu = dt("wu", [22, 128, 1024])
    d.wd = dt("wd", [2, 8, 128, 11 * 128])
    d.x2T = dt("x2T", [D, T], F32, "ExternalOutput")
    d.ydbg = dt("ydbg", [8, 128, T], BF16, "ExternalOutput")

    P = Prog(nc)
    C = NS()
    AR = Arena(P, 211000)
    al = AR.alloc
    C.pb = [P.ps("pb%d" % i) for i in range(7)]
    C.ptb = P.ps("ptb", [128, 1024], BF16)
    C.ones = al([128, 128])
    C.bd = al([128, 128])
    C.ident = al([128, 128])
    C.nu = al([128, 128])
    C.negone = al([1, 128])
    C.ynsT = al([128, 4, T])
    C.ysbT = al([128, 2, T])
    C.yswT = al([128, 2, T])
    P.op("dve", lambda e: e.memset(C.ones, 1.0), w=["ones"])
    P.op("dve", lambda e: e.memset(C.bd, 0.0), w=["bd"])
    P.op("dve", lambda e: e.memset(C.bd[0:64, 0:64], 1.0), r=["bd"], w=["bd"])
    P.op("dve", lambda e: e.memset(C.bd[64:128, 64:128], 1.0), r=["bd"], w=["bd"])
    P.op("dve", lambda e: e.memset(C.ident, 1.0), w=["ident"])
    P.op("pool", lambda e: e.affine_select(out=C.ident, in_=C.ident, pattern=[[-1, 128]], compare_op=ALU.is_equal,
                                           fill=0.0, base=0, channel_multiplier=1), r=["ident"], w=["ident"])
    C.identf = al([128, 128], F32)
    P.op("dve", lambda e: e.memset(C.identf, 1.0), w=["identf"])
    P.op("pool", lambda e: e.affine_select(out=C.identf, in_=C.identf, pattern=[[-1, 128]], compare_op=ALU.is_equal,
                                           fill=0.0, base=0, channel_multiplier=1), r=["identf"], w=["identf"])
    P.op("dve", lambda e: e.memset(C.nu, -1.0), w=["nu"])
    P.op("pool", lambda e: e.affine_select(out=C.nu, in_=C.nu, pattern=[[-1, 128]], compare_op=ALU.is_ge,
                                           fill=0.0, base=0, channel_multiplier=1), r=["nu"], w=["nu"])
    P.op("dve", lambda e: e.memset(C.negone, -1.0), w=["negone"])
    C.negident = al([128, 128])
    P.op("dve", lambda e: e.memset(C.negident, -1.0), w=["negident"])
    P.op("pool", lambda e: e.affine_select(out=C.negident, in_=C.negident, pattern=[[-1, 128]], compare_op=ALU.is_equal,
                                           fill=0.0, base=0, channel_multiplier=1), r=["negident"], w=["negident"])
    for nm_, t_ in (("ynsT", C.ynsT), ("ysbT", C.ysbT), ("yswT", C.yswT)):
        if {"ynsT": "nsa", "ysbT": "sb", "yswT": "swa"}[nm_] not in phases:
            P.op("dve", lambda e, t_=t_: e.memset(t_.rearrange("p c t -> p (c t)"), 0.0),
                 w=[(nm_[:3], i) for i in range(16)] + [(nm_[:3], a, b) for a in range(2) for b in range(4)])
    if "sb" in phases:
        phase_sb(P, C, AR, d)
    if "swa" in phases:
        phase_swa(P, C, AR, d)
    if "nsa" in phases:
        phase_nsa(P, C, AR, d)
    ytoks = ([("yns", i) for i in range(16)] + [("ysb", cc, sbk) for cc in range(2) for sbk in range(4)]
             + [("ysw", i) for i in range(16)])
    for c in range(4):
        P.dma("sp", lambda e, c=c: e.dma_start(out=d.ydbg[c], in_=C.ynsT[:, c, :]), r=ytoks)
    for c in range(2):
        P.dma("sp", lambda e, c=c: e.dma_start(out=d.ydbg[4 + c], in_=C.ysbT[:, c, :]), r=ytoks)
        P.dma("sp", lambda e, c=c: e.dma_start(out=d.ydbg[6 + c], in_=C.yswT[:, c, :]), r=ytoks)
    if "merge" in phases:
        phase_merge_ffn2(P, C, AR, d)
    P.finish()
    P.emit()
    return nc


BF = ml_dtypes.bfloat16
QCH = [0, 1, 2, 3, 8, 9, 12, 13]
KCH = [4, 5, 6, 10, 11]


def t5_bucket_np(dd):
    dd = np.maximum(dd, 0)
    df = np.maximum(dd, 1).astype(np.float32)
    large = 16 + (np.log(df / np.float32(16)) / np.float32(math.log(8.0)) * np.float32(16)).astype(np.int32)
    large = np.minimum(large, 31)
    return np.where(dd < 16, dd, large)


def bias_tab(rel_bias, dist, col, valid):
    return np.where(valid, rel_bias[t5_bucket_np(dist), col], np.float32(NEG)).astype(np.float32)


def core_tables(rel_bias, rc):
    tq = np.arange(128)[None, :]
    sk = np.arange(128)[:, None]
    t = {}
    bs = np.zeros((128, 2, 5, 4, 128), np.float32)
    bw = np.zeros((128, 2, 3, 4, 128), np.float32)
    bsw = np.zeros((128, 2, 2, 2, 128), np.float32)
    bc = np.full((32, 2, 4, 128), NEG, np.float32)
    for g in range(2):
        for r in range(4):
            col = g * 4 + r
            for jj in range(5):
                dist = (rc - (jj - 1)) * 128 + tq - sk
                bs[:, g, jj, r, :] = bias_tab(rel_bias, dist, col, dist >= 0)
            for bi, dl in enumerate((4, 1, 0)):
                dist = dl * 128 + tq - sk
                bw[:, g, bi, r, :] = bias_tab(rel_bias, dist, col, (dist >= 0) & (dist < 512))
            for m in range(31):
                dd = tq[0] - 16 * (m - 24) - 31
                bc[m, g, r, :] = bias_tab(rel_bias, dd, col, dd >= 0)
        for r in range(2):
            col = 8 + g * 2 + r
            for di in range(2):
                dist = (1 - di) * 128 + tq - sk
                bsw[:, di, g, r, :] = bias_tab(rel_bias, dist, col, (dist >= 0) & (dist < 128))
    t["bs"] = bs.reshape(128, -1)
    t["bw"] = bw.reshape(128, -1)
    t["bsw"] = bsw.reshape(128, -1)
    t["bc"] = bc.reshape(32, -1)
    sel = np.zeros((32, 16, 2, 128), np.float32)
    nk = np.arange(128)
    for i in range(16):
        gt = 4 * i + rc
        ctmax = i // 4
        for slot in range(2):
            ct = ctmax - 1 + slot
            if ct < 0:
                continue
            n = 128 * ct + nk
            rel = n - 8 * gt
            m = np.where((rel >= 7) | (n >= 511), 31, np.where(rel >= -24, rel + 24, -1))
            ok = m >= 0
            sel[m[ok], i, slot, nk[ok]] = 1.0
    t["sel"] = sel.reshape(32, -1)
    u = np.arange(248)[None, :]
    c0 = (np.arange(128)[:, None] >= 64).astype(np.int64)
    rel = u - 120 - 2 * rc
    t["at"] = (rel <= c0 - 2).astype(np.float32)
    t["ct"] = np.where((rel == c0) | (rel == c0 - 1), np.float32(1e4),
                       np.where(rel > c0, np.float32(-1.0), np.float32(0.0))).astype(np.float32)
    msb = np.zeros((128, 4, 128), np.float32)
    for j in range(4):
        if j < rc:
            msb[:, j, :] = 1.0
        elif j == rc:
            msb[:, j, :] = (sk < tq)
    t["msb"] = msb.astype(BF)
    return t


def const_tables():
    fmask = np.where((np.arange(S)[None, :] // 64) == np.arange(128)[:, None], np.float32(NEG), np.float32(0)).astype(BF)
    ovl = np.zeros((128, 4, 129), np.float32)
    for ct in range(4):
        n = 128 * ct + np.arange(128)
        j = np.arange(128)
        ok = (n[:, None] < 511) & (16 * n[:, None] < 64 * j[None, :] + 64) & (16 * n[:, None] + 31 >= 64 * j[None, :])
        ovl[:, ct, 1:] = ok
        ovl[:, ct, 0] = (n < 511)
    return dict(fmask=fmask, ovl=ovl.astype(BF))


def gather_global(resA, b):
    fm = np.stack([np.asarray(resA[b * 4 + r]["fm"]).reshape(15, 128, 16, 128) for r in range(4)], axis=3).reshape(15, 128, S)
    tm = np.stack([np.asarray(resA[b * 4 + r]["tm"]).reshape(16, 128, 640) for r in range(4)], axis=1).reshape(S, 640)
    return fm, tm


def tile_or_zero(arr, tile, axis):
    if tile < 0:
        shp = list(arr.shape)
        shp[axis] = 128
        return np.zeros(shp, arr.dtype)
    sl = [slice(None)] * arr.ndim
    sl[axis] = slice(tile * 128, (tile + 1) * 128)
    return arr[tuple(sl)]


def vaug(v):
    o = np.ones((128, 2, 65), v.dtype)
    o[:, :, :64] = v.reshape(128, 2, 64)
    return o


def prep_B_batch(fm, tm):
    sh = {}
    sh["kt"] = np.ascontiguousarray(fm[KCH])
    sv = tm[:, 256:512]
    sh["vsb"] = np.ascontiguousarray(sv.reshape(64, 128, 2, 128).transpose(2, 1, 0, 3)).reshape(2, 128, 64 * 128)
    nvs = tm[:, 0:128].reshape(64, 128, 2, 64)
    vsl = np.ones((128, 64, 2, 65), tm.dtype)
    vsl[:, :, :, :64] = nvs.transpose(1, 0, 2, 3)
    sh["vsl"] = vsl.reshape(128, -1)
    return sh


def prep_B_core(fm, tm, rc, resA_c, x1T):
    o = {}
    nkw, wk = fm[7], fm[14]
    nvw, wv = tm[:, 128:256], tm[:, 512:640]
    kwn = np.zeros((16, 128, 640), fm.dtype)
    vwn = np.zeros((16, 128, 5, 2, 65), tm.dtype)
    kw2 = np.zeros((128, 16, 256), fm.dtype)
    vw2 = np.zeros((128, 16, 2, 2, 65), tm.dtype)
    for i in range(16):
        gt = 4 * i + rc
        for di in range(5):
            tl = gt - 4 + di
            if tl >= 0:
                kwn[i, :, di * 128:(di + 1) * 128] = nkw[:, tl * 128:(tl + 1) * 128]
                vwn[i, :, di] = vaug(nvw[tl * 128:(tl + 1) * 128])
        for di in range(2):
            tl = gt - 1 + di
            if tl >= 0:
                kw2[:, i, di * 128:(di + 1) * 128] = wk[:, tl * 128:(tl + 1) * 128]
                vw2[:, i, di] = vaug(wv[tl * 128:(tl + 1) * 128])
    o["kwn"] = kwn
    o["vwn"] = vwn.reshape(16, 128, 650)
    o["kw2"] = kw2
    o["vw2"] = vw2.reshape(128, -1)
    o["fmq"] = np.ascontiguousarray(np.asarray(resA_c["fm"])[QCH])
    o["tmg"] = np.ascontiguousarray(np.asarray(resA_c["tmg"]).reshape(16, 128, 24).transpose(1, 0, 2)).reshape(128, -1)
    o["x1T"] = x1T
    return o


def prep_B_weights(inp, l):
    w = {}
    rep = lambda a: np.ascontiguousarray(np.concatenate([a, a], axis=0))
    for nm_, key in (("w1k", "nsa_cmp_k_w1"), ("w1v", "nsa_cmp_v_w1")):
        w1 = inp[key][l].reshape(32, 64, 256).transpose(1, 0, 2).reshape(64, 32 * 256)
        w[nm_] = rep(w1)
    for nm_, key in (("w2k", "nsa_cmp_k_w2"), ("w2v", "nsa_cmp_v_w2")):
        w[nm_] = np.ascontiguousarray(inp[key][l].reshape(2, 128, 64).transpose(1, 0, 2)).reshape(128, 128)
    w["post"] = rep(np.ascontiguousarray(inp["nsa_cmp_pos"][l].T))
    w["gk"] = rep(inp["nsa_k_norm"][l].reshape(64, 1))
    w["nm"] = vec8(inp["mix_norm"][l])
    w["n2"] = vec8(inp["ffn2_norm"][l])
    w_in = inp["w_in"][l]
    bg = w_in[:, O_BG:].reshape(8, 128, 3, 8, 128)
    w["wbg"] = np.ascontiguousarray(bg.transpose(3, 2, 1, 0, 4)).reshape(8, 3, 128, 1024)
    for nm_, key, nkc in (("wupn", "w_up_nsa", 4), ("wups", "w_up_sb", 2), ("wupw", "w_up_swa", 2)):
        u = inp[key][l].reshape(nkc, 128, 8, 128)
        w[nm_] = np.ascontiguousarray(u.transpose(2, 1, 0, 3)).reshape(8, 128, nkc * 128)
    wo = inp["w_out"][l].reshape(8, 128, 8, 128)
    w["wo"] = np.ascontiguousarray(wo.transpose(2, 1, 0, 3)).reshape(8, 128, 1024)
    w["wg"], w["wu"], w["wd"] = prep_ffn(inp["ffn2_w_gate"][l], inp["ffn2_w_up"][l], inp["ffn2_w_down"][l])
    w["b31"] = np.ascontiguousarray(np.broadcast_to(inp["rel_bias"][31, 0:8], (128, 8)))
    w["sinks"] = np.ascontiguousarray(np.broadcast_to(inp["swa_sinks"][l], (128, 4)))
    return w


_CACHE = {}


def get_nc(name):
    if name not in _CACHE:
        _CACHE[name] = build_A() if name == "A" else build_B()
    return _CACHE[name]


def run_layer(inp, l, xTs, tabs, consts):
    ncA = get_nc("A")
    pa = prep_A(inp, l)
    resA = run_bass_kernel_spmd(ncA, [dict(pa, xT=xTs[c]) for c in range(8)], core_ids=list(range(8))).results
    wB = prep_B_weights(inp, l)
    maps = []
    for b in range(2):
        fm, tm = gather_global(resA, b)
        sh = prep_B_batch(fm, tm)
        for rc in range(4):
            c = b * 4 + rc
            m = dict(wB)
            m.update(consts)
            m.update(tabs[rc])
            m.update(sh)
            m.update(prep_B_core(fm, tm, rc, resA[c], np.asarray(resA[c]["x1T"])))
            maps.append(m)
    ncB = get_nc("B")
    resB = run_bass_kernel_spmd(ncB, maps, core_ids=list(range(8))).results
    return [np.asarray(resB[c]["x2T"]) for c in range(8)], resA, resB


def kernel(**inp):
    inp = {k: np.asarray(v) for k, v in inp.items()}
    xTs = shard_xT(inp["x"].astype(np.float32))
    tabs = [core_tables(inp["rel_bias"], rc) for rc in range(4)]
    consts = const_tables()
    for l in range(2):
        xTs, _, _ = run_layer(inp, l, xTs, tabs, consts)
    return unshard_xT(xTs)
```
